# Optimizing a Trainium2 kernel written in Bass

```python
import jax
import jax.numpy as jnp
from jax import lax
import numpy as np

D_MODEL = 1024
BATCH = 8
SEQ = 2048
DEPTH = 2
DEC_BATCH = 128
DEC_SEQ = 1
PAST_LEN = 16384
PAGE_SIZE = 128

HEAD_DIM = 64
GROUP_WIDTH = D_MODEL // 4
MIX_WIDTH = 4 * GROUP_WIDTH
N_DELTA_HEADS = GROUP_WIDTH // HEAD_DIM
N_RET_HEADS = GROUP_WIDTH // HEAD_DIM
POOL_WINDOWS = (2, 4, 8, 16)
POOL_GROUP = GROUP_WIDTH // len(POOL_WINDOWS)
POOL_BUF = max(POOL_WINDOWS) - 1
DELTA_CONV = 4
CONF_CONV = 31
CHUNK = 64
ROPE_BASE = 10000.0
EPS = 1e-6
OFF_DQKV = 0
OFF_DZ = OFF_DQKV + 3 * GROUP_WIDTH
OFF_DA = OFF_DZ + GROUP_WIDTH
OFF_DB = OFF_DA + N_DELTA_HEADS
OFF_PU = OFF_DB + N_DELTA_HEADS
OFF_RQ = OFF_PU + GROUP_WIDTH
OFF_RK = OFF_RQ + GROUP_WIDTH
OFF_RV = OFF_RK + GROUP_WIDTH
OFF_RG = OFF_RV + GROUP_WIDTH
OFF_CG = OFF_RG + GROUP_WIDTH
IN_WIDTH = OFF_CG + 2 * GROUP_WIDTH
PEER_HEADS = 8
PEER_NKEYS = 128
PEER_EXPERTS = PEER_NKEYS * PEER_NKEYS
PEER_TOPK = 16
PEER_DKEY = 256
PEER_DHALF = PEER_DKEY // 2
PEER_BLOCK = 256

kernel_name = "hybrid_delta_pool_retention_conformer_peer_step"


def rmsnorm(x, w):
    xf = x.astype(jnp.float32)
    y = xf * lax.rsqrt(jnp.mean(xf * xf, axis=-1, keepdims=True) + EPS)
    return (y * w.astype(jnp.float32)).astype(x.dtype)


def causal_dwconv(u, buf, w):
    full = jnp.concatenate([buf.astype(u.dtype), u], axis=1)
    y = lax.conv_general_dilated(full, w[:, None, :].astype(u.dtype), window_strides=(1,),
                                 padding='VALID', dimension_numbers=('NWC', 'WIO', 'NWC'),
                                 feature_group_count=u.shape[-1])
    return y, full[:, full.shape[1] - (w.shape[0] - 1):]


def _chunk(x, c):
    b, l = x.shape[:2]
    n = -(-l // c)
    x = jnp.pad(x, [(0, 0), (0, n * c - l)] + [(0, 0)] * (x.ndim - 2))
    x = x.reshape((b, n, c) + x.shape[2:])
    return jnp.transpose(x, (1, 0, 3, 2) + tuple(range(4, x.ndim)))


def _unchunk(o, l):
    n, b, h, c, d = o.shape
    return jnp.transpose(o, (1, 0, 3, 2, 4)).reshape(b, n * c, h, d)[:, :l]


def _decay_masks(g, c):
    G = jnp.cumsum(g, axis=-1)
    idx = jnp.arange(c)
    incl = idx[:, None] >= idx[None, :]
    diff = G[..., :, None] - G[..., None, :]
    dmask = jnp.exp(jnp.where(incl, diff, -jnp.inf))
    return G, dmask, idx


def gated_delta_rule(q, k, v, beta, g, S0):
    l = q.shape[1]
    c = min(CHUNK, l)
    q, k, v, beta, g = (_chunk(t, c) for t in (q, k, v, beta, g))
    G, dmask, idx = _decay_masks(g, c)
    strict = idx[:, None] > idx[None, :]
    kb = k * beta[..., None]
    A = jnp.where(strict, jnp.einsum('...id,...jd->...ij', kb, k) * dmask, 0.0)
    eye = jnp.eye(c, dtype=A.dtype)
    T = lax.linalg.triangular_solve(eye + A, jnp.broadcast_to(eye, A.shape), left_side=True, lower=True)
    eG = jnp.exp(G)[..., None]
    U = jnp.einsum('...ij,...jd->...id', T, v * beta[..., None])
    W = jnp.einsum('...ij,...jd->...id', T, kb * eG)
    Qd = q * eG
    Aqk = jnp.einsum('...id,...jd->...ij', q, k) * dmask
    Glast = G[..., -1:]
    Kd = k * jnp.exp(Glast - G)[..., None]
    dlast = jnp.exp(Glast[..., 0])

    def step(S, xs):
        U_i, W_i, Qd_i, A_i, Kd_i, dl_i = xs
        v_new = U_i - jnp.einsum('bhcd,bhde->bhce', W_i, S)
        o = jnp.einsum('bhcd,bhde->bhce', Qd_i, S) + jnp.einsum('bhij,bhje->bhie', A_i, v_new)
        S = S * dl_i[..., None, None] + jnp.einsum('bhcd,bhce->bhde', Kd_i, v_new)
        return S, o

    S, o = lax.scan(step, S0, (U, W, Qd, Aqk, Kd, dlast))
    return _unchunk(o, l), S


def decayed_linear_attn(q, k, v, g, S0):
    l = q.shape[1]
    c = min(CHUNK, l)
    q, k, v, g = (_chunk(t, c) for t in (q, k, v, g))
    G, dmask, _ = _decay_masks(g, c)
    Aqk = jnp.einsum('...id,...jd->...ij', q, k) * dmask
    Qd = q * jnp.exp(G)[..., None]
    Glast = G[..., -1:]
    Kd = k * jnp.exp(Glast - G)[..., None]
    dlast = jnp.exp(Glast[..., 0])

    def step(S, xs):
        A_i, V_i, Qd_i, Kd_i, dl_i = xs
        o = jnp.einsum('bhcd,bhde->bhce', Qd_i, S) + jnp.einsum('bhij,bhje->bhie', A_i, V_i)
        S = S * dl_i[..., None, None] + jnp.einsum('bhcd,bhce->bhde', Kd_i, V_i)
        return S, o

    S, o = lax.scan(step, S0, (Aqk, v, Qd, Kd, dlast))
    return _unchunk(o, l), S


def rotary(x, pos):
    half = x.shape[-1] // 2
    inv = 1.0 / (ROPE_BASE ** (jnp.arange(half, dtype=jnp.float32) / half))
    ang = pos[:, None] * inv[None, :]
    cos = jnp.cos(ang)[None, :, None, :]
    sin = jnp.sin(ang)[None, :, None, :]
    x1, x2 = x[..., :half], x[..., half:]
    return jnp.concatenate([x1 * cos - x2 * sin, x1 * sin + x2 * cos], axis=-1)


def l2norm(x):
    return x * lax.rsqrt(jnp.sum(x * x, axis=-1, keepdims=True) + EPS)


def delta_mixer(z, s_dconv, s_delta, conv_w, a_log, dt_bias, norm_w):
    b, l, _ = z.shape
    H, dh = N_DELTA_HEADS, HEAD_DIM
    f32 = jnp.float32
    qkv, new_buf = causal_dwconv(z[..., OFF_DQKV:OFF_DQKV + 3 * GROUP_WIDTH], s_dconv, conv_w)
    qkv = jax.nn.silu(qkv.astype(f32))
    q = l2norm(qkv[..., :GROUP_WIDTH].reshape(b, l, H, dh)) * (dh ** -0.5)
    k = l2norm(qkv[..., GROUP_WIDTH:2 * GROUP_WIDTH].reshape(b, l, H, dh))
    v = qkv[..., 2 * GROUP_WIDTH:].reshape(b, l, H, dh)
    a = z[..., OFF_DA:OFF_DA + H].astype(f32)
    beta = jax.nn.sigmoid(z[..., OFF_DB:OFF_DB + H].astype(f32))
    g = -jnp.exp(a_log.astype(f32)) * jax.nn.softplus(a + dt_bias.astype(f32))
    o, S = gated_delta_rule(q, k, v, beta, g, s_delta.astype(f32))
    o = o * lax.rsqrt(jnp.mean(o * o, axis=-1, keepdims=True) + EPS) * norm_w.astype(f32)
    o = o.reshape(b, l, GROUP_WIDTH) * jax.nn.silu(z[..., OFF_DZ:OFF_DZ + GROUP_WIDTH].astype(f32))
    return o.astype(z.dtype), new_buf, S


def pool_mixer(z, s_pool, start_pos, pool_w, pool_scale):
    u = z[..., OFF_PU:OFF_PU + GROUP_WIDTH]
    b, l, _ = u.shape
    full = jnp.concatenate([s_pool.astype(u.dtype), u], axis=1)
    new_buf = full[:, l:]
    ff = full.astype(jnp.float32)
    cs = jnp.concatenate([jnp.zeros((b, 1, GROUP_WIDTH), jnp.float32), jnp.cumsum(ff, axis=1)], axis=1)
    t = jnp.arange(l)
    means = []
    for gi, w in enumerate(POOL_WINDOWS):
        sl = slice(gi * POOL_GROUP, (gi + 1) * POOL_GROUP)
        wsum = cs[:, POOL_BUF + 1:POOL_BUF + 1 + l, sl] - cs[:, POOL_BUF + 1 - w:POOL_BUF + 1 - w + l, sl]
        cnt = jnp.minimum(w, start_pos + t + 1).astype(jnp.float32)
        means.append(wsum / cnt[None, :, None])
    p = (jnp.concatenate(means, axis=-1) - ff[:, POOL_BUF:]).reshape(b, l, len(POOL_WINDOWS), POOL_GROUP)
    y = jnp.einsum('blgc,gcd->blgd', p, pool_w.astype(jnp.float32)).reshape(b, l, GROUP_WIDTH)
    y = y * pool_scale.astype(jnp.float32)
    return y.astype(z.dtype), new_buf


def retention_mixer(z, s_ret, start_pos):
    b, l, _ = z.shape
    H, dh = N_RET_HEADS, HEAD_DIM
    f32 = jnp.float32
    q = z[..., OFF_RQ:OFF_RQ + GROUP_WIDTH].astype(f32).reshape(b, l, H, dh)
    k = z[..., OFF_RK:OFF_RK + GROUP_WIDTH].astype(f32).reshape(b, l, H, dh)
    v = z[..., OFF_RV:OFF_RV + GROUP_WIDTH].astype(f32).reshape(b, l, H, dh)
    gate = z[..., OFF_RG:OFF_RG + GROUP_WIDTH].astype(f32)
    pos = jnp.arange(l, dtype=f32) + start_pos
    q = rotary(q, pos)
    k = rotary(k, pos) * (dh ** -0.5)
    log_gamma = jnp.log1p(-jnp.exp2(-5.0 - jnp.arange(H, dtype=f32)))
    g = jnp.broadcast_to(log_gamma, (b, l, H))
    o, S = decayed_linear_attn(q, k, v, g, s_ret.astype(f32))
    mu = jnp.mean(o, axis=-1, keepdims=True)
    var = jnp.mean(jnp.square(o - mu), axis=-1, keepdims=True)
    o = (o - mu) * lax.rsqrt(var + EPS)
    o = jax.nn.silu(gate) * o.reshape(b, l, GROUP_WIDTH)
    return o.astype(z.dtype), S


def conformer_conv_mixer(z, s_conv, dw_w, dw_b, ln_w, ln_b, pw_w):
    f32 = jnp.float32
    a = z[..., OFF_CG:OFF_CG + GROUP_WIDTH]
    gt = z[..., OFF_CG + GROUP_WIDTH:OFF_CG + 2 * GROUP_WIDTH]
    glu = a * jax.nn.sigmoid(gt)
    dc, new_buf = causal_dwconv(glu, s_conv, dw_w)
    dc = dc.astype(f32) + dw_b.astype(f32)
    mu = jnp.mean(dc, axis=-1, keepdims=True)
    var = jnp.mean(jnp.square(dc - mu), axis=-1, keepdims=True)
    hn = (dc - mu) * lax.rsqrt(var + EPS) * ln_w.astype(f32) + ln_b.astype(f32)
    y = jnp.einsum('blc,cd->bld', jax.nn.silu(hn).astype(z.dtype), pw_w)
    return y.astype(z.dtype), new_buf


def peer_ffn(h, wq, keys, U, V):
    b, l, d = h.shape
    n_tok = b * l
    blk = min(PEER_BLOCK, n_tok)
    nb = -(-n_tok // blk)
    hb = jnp.pad(h.reshape(n_tok, d), ((0, nb * blk - n_tok), (0, 0))).reshape(nb, blk, d)
    K = PEER_TOPK

    def one(xb):
        q = jnp.einsum('td,de->te', xb, wq).astype(jnp.float32).reshape(blk, PEER_HEADS, 2, PEER_DHALF)
        s = jnp.einsum('thpc,hpnc->thpn', q, keys.astype(jnp.float32))
        s_top, i_top = lax.top_k(s, K)
        cand = s_top[:, :, 0, :, None] + s_top[:, :, 1, None, :]
        cidx = i_top[:, :, 0, :, None] * PEER_NKEYS + i_top[:, :, 1, None, :]
        best, sel = lax.top_k(cand.reshape(blk, PEER_HEADS, K * K), K)
        eidx = jnp.take_along_axis(cidx.reshape(blk, PEER_HEADS, K * K), sel, axis=-1)
        gate = jax.nn.softmax(best, axis=-1)
        ue = U[eidx]
        ve = V[eidx]
        act = jax.nn.gelu(jnp.einsum('td,thkd->thk', xb, ue).astype(jnp.float32))
        w = (gate * act).astype(xb.dtype)
        return jnp.einsum('thk,thkd->td', w, ve)

    out = lax.map(one, hb).reshape(nb * blk, d)[:n_tok]
    return out.reshape(b, l, d).astype(h.dtype)


def decoder_layer(x, s_delta, s_dconv, s_pool, s_ret, s_conv, p, start_pos):
    h = rmsnorm(x, p['norm1'])
    z = jnp.einsum('bld,de->ble', h, p['w_in'])
    o_a, n_dconv, n_delta = delta_mixer(z, s_dconv, s_delta, p['delta_conv_w'], p['delta_a_log'],
                                        p['delta_dt_bias'], p['delta_norm_w'])
    o_b, n_pool = pool_mixer(z, s_pool, start_pos, p['pool_w'], p['pool_scale'])
    o_c, n_ret = retention_mixer(z, s_ret, start_pos)
    o_d, n_conv = conformer_conv_mixer(z, s_conv, p['conv_dw_w'], p['conv_dw_b'], p['conv_ln_w'],
                                       p['conv_ln_b'], p['conv_pw_w'])
    mix = jnp.concatenate([o_a, o_b, o_c, o_d], axis=-1)
    x = x + jnp.einsum('ble,ed->bld', mix, p['w_out'])
    x = x + peer_ffn(rmsnorm(x, p['norm2']), p['peer_wq'], p['peer_keys'], p['peer_u'], p['peer_v'])
    return x, (n_delta, n_dconv, n_pool, n_ret, n_conv)


def run_trunk(x, s_delta, s_dconv, s_pool, s_ret, s_conv, layer_params, final_norm, start_pos):
    new = ([], [], [], [], [])
    for li in range(DEPTH):
        p_l = {name: arr[li] for name, arr in layer_params.items()}
        x, st = decoder_layer(x, s_delta[li], s_dconv[li], s_pool[li], s_ret[li], s_conv[li], p_l, start_pos)
        for acc, a in zip(new, st):
            acc.append(a.astype(x.dtype))
    return rmsnorm(x, final_norm), tuple(jnp.stack(acc) for acc in new)


def setup_inputs(seed: int = 0) -> dict:
    key = jax.random.key(seed)
    ks = jax.random.split(key, 28)
    f32 = jnp.float32

    def nrm(k, shape, scale):
        return jax.random.normal(k, shape, f32) * scale

    H = N_DELTA_HEADS
    dt = jnp.exp(jax.random.uniform(ks[11], (DEPTH, H), f32, np.log(1e-3), np.log(1e-1)))
    return {
        'x_prompt': nrm(ks[0], (BATCH, SEQ, D_MODEL), 1.0),
        'x_sample': nrm(ks[1], (DEC_BATCH, DEC_SEQ, D_MODEL), 1.0),
        'state_delta': nrm(ks[2], (DEPTH, DEC_BATCH, N_DELTA_HEADS, HEAD_DIM, HEAD_DIM), 0.1),
        'state_delta_conv': nrm(ks[3], (DEPTH, DEC_BATCH, DELTA_CONV - 1, 3 * GROUP_WIDTH), 1.0),
        'state_pool': nrm(ks[4], (DEPTH, DEC_BATCH, POOL_BUF, GROUP_WIDTH), 1.0),
        'state_ret': nrm(ks[5], (DEPTH, DEC_BATCH, N_RET_HEADS, HEAD_DIM, HEAD_DIM), 0.3),
        'state_conv': nrm(ks[6], (DEPTH, DEC_BATCH, CONF_CONV - 1, GROUP_WIDTH), 0.5),
        'norm1': 1.0 + nrm(ks[7], (DEPTH, D_MODEL), 0.02),
        'w_in': nrm(ks[8], (DEPTH, D_MODEL, IN_WIDTH), D_MODEL ** -0.5),
        'delta_conv_w': nrm(ks[9], (DEPTH, DELTA_CONV, 3 * GROUP_WIDTH), DELTA_CONV ** -0.5),
        'delta_a_log': jnp.log(jax.random.uniform(ks[10], (DEPTH, H), f32, 1.0, 16.0)),
        'delta_dt_bias': jnp.log(jnp.expm1(dt)),
        'delta_norm_w': 1.0 + nrm(ks[12], (DEPTH, HEAD_DIM), 0.02),
        'pool_w': nrm(ks[13], (DEPTH, len(POOL_WINDOWS), POOL_GROUP, POOL_GROUP), POOL_GROUP ** -0.5),
        'pool_scale': 1.0 + nrm(ks[14], (DEPTH, GROUP_WIDTH), 0.02),
        'conv_dw_w': nrm(ks[15], (DEPTH, CONF_CONV, GROUP_WIDTH), CONF_CONV ** -0.5),
        'conv_dw_b': nrm(ks[16], (DEPTH, GROUP_WIDTH), 0.02),
        'conv_ln_w': 1.0 + nrm(ks[17], (DEPTH, GROUP_WIDTH), 0.02),
        'conv_ln_b': nrm(ks[18], (DEPTH, GROUP_WIDTH), 0.02),
        'conv_pw_w': nrm(ks[19], (DEPTH, GROUP_WIDTH, GROUP_WIDTH), GROUP_WIDTH ** -0.5),
        'w_out': nrm(ks[20], (DEPTH, MIX_WIDTH, D_MODEL), 0.5 * MIX_WIDTH ** -0.5),
        'norm2': 1.0 + nrm(ks[21], (DEPTH, D_MODEL), 0.02),
        'peer_wq': nrm(ks[22], (DEPTH, D_MODEL, PEER_HEADS * PEER_DKEY), D_MODEL ** -0.5),
        'peer_keys': nrm(ks[23], (DEPTH, PEER_HEADS, 2, PEER_NKEYS, PEER_DHALF), PEER_DHALF ** -0.5),
        'peer_u': nrm(ks[24], (DEPTH, PEER_EXPERTS, D_MODEL), D_MODEL ** -0.5),
        'peer_v': nrm(ks[25], (DEPTH, PEER_EXPERTS, D_MODEL), (PEER_HEADS * PEER_TOPK) ** -0.5),
        'final_norm': 1.0 + nrm(ks[26], (D_MODEL,), 0.02),
    }


def reference(x_prompt, x_sample, state_delta, state_delta_conv, state_pool, state_ret, state_conv,
              norm1, w_in, delta_conv_w, delta_a_log, delta_dt_bias, delta_norm_w, pool_w, pool_scale,
              conv_dw_w, conv_dw_b, conv_ln_w, conv_ln_b, conv_pw_w, w_out, norm2,
              peer_wq, peer_keys, peer_u, peer_v, final_norm):
    layer_params = {
        'norm1': norm1, 'w_in': w_in, 'delta_conv_w': delta_conv_w, 'delta_a_log': delta_a_log,
        'delta_dt_bias': delta_dt_bias, 'delta_norm_w': delta_norm_w, 'pool_w': pool_w,
        'pool_scale': pool_scale, 'conv_dw_w': conv_dw_w, 'conv_dw_b': conv_dw_b,
        'conv_ln_w': conv_ln_w, 'conv_ln_b': conv_ln_b, 'conv_pw_w': conv_pw_w, 'w_out': w_out,
        'norm2': norm2, 'peer_wq': peer_wq, 'peer_keys': peer_keys, 'peer_u': peer_u, 'peer_v': peer_v,
    }
    dt = x_prompt.dtype
    bp = x_prompt.shape[0]
    z_delta = jnp.zeros((DEPTH, bp) + state_delta.shape[2:], dt)
    z_dconv = jnp.zeros((DEPTH, bp) + state_delta_conv.shape[2:], dt)
    z_pool = jnp.zeros((DEPTH, bp) + state_pool.shape[2:], dt)
    z_ret = jnp.zeros((DEPTH, bp) + state_ret.shape[2:], dt)
    z_conv = jnp.zeros((DEPTH, bp) + state_conv.shape[2:], dt)
    y_prompt, (d_p, dc_p, pl_p, r_p, c_p) = run_trunk(
        x_prompt, z_delta, z_dconv, z_pool, z_ret, z_conv, layer_params, final_norm, 0)
    y_sample, (d_s, dc_s, pl_s, r_s, c_s) = run_trunk(
        x_sample, state_delta, state_delta_conv, state_pool, state_ret, state_conv,
        layer_params, final_norm, PAST_LEN)
    return (y_prompt, y_sample, d_p, dc_p, pl_p, r_p, c_p, d_s, dc_s, pl_s, r_s, c_s)
```

```python
import numpy as np
import concourse.bass as bass
import concourse.mybir as mybir
from concourse.bass_utils import run_bass_kernel_spmd

F32 = mybir.dt.float32
I32 = mybir.dt.int32
U32 = mybir.dt.uint32
BF16 = mybir.dt.bfloat16
ALU = mybir.AluOpType
AF = mybir.ActivationFunctionType
AX = mybir.AxisListType

D_MODEL = 1024
SEQ = 2048
DEPTH = 2
NS = 16
NCORES = 8
IN_W = 2824
EPS = 1e-6
NEG = -1.0e30
DEBUG = {}
NO_DRAIN = True
DELTA_PRIO = 3


class Buf:
    __slots__ = ("w", "r", "bank")

    def __init__(self):
        self.w = None
        self.r = {}
        self.bank = None


class T:
    __slots__ = ("ap", "buf")

    def __init__(self, ap, buf=None):
        self.ap = ap
        self.buf = buf if buf is not None else Buf()

    def __getitem__(self, key):
        return T(self.ap[key], self.buf)

    def bc(self, shape):
        return T(self.ap.to_broadcast(list(shape)), self.buf)

    def un(self, axis):
        return T(self.ap.unsqueeze(axis), self.buf)

    def r(self, pat, **kw):
        return T(self.ap.rearrange(pat, **kw), self.buf)

    def bitcast(self, dt):
        return T(self.ap.bitcast(dt), self.buf)


def _ap(x):
    return x.ap if isinstance(x, T) else x


class KB:
    NDMA = 24

    def __init__(self, nc):
        self.nc = nc
        self.eng = dict(pe=nc.tensor, dve=nc.vector, act=nc.scalar, pool=nc.gpsimd, sp=nc.sync)
        self.sem = {("e", e): nc.alloc_semaphore("sem_" + e) for e in self.eng}
        self.cnt = {e: 0 for e in self.eng}
        self.seen = {e: {} for e in self.eng}
        self.dq = {}
        for q in ("sp", "pool"):
            sl = []
            for i in range(self.NDMA):
                key = ("d", q, i)
                self.sem[key] = nc.alloc_semaphore("dma_%s_%d" % (q, i))
                sl.append(key)
            self.dq[q] = dict(slots=sl, uses=[0] * self.NDMA, i=0)
        for b in range(8):
            self.sem[("p", b)] = nc.alloc_semaphore("sem_pe_bank%d" % b)
        self.pe_cnt = [0] * 8
        self.out_events = []
        self.ninst = 0

    def _waits(self, eng, reads, writes, self_sync):
        evs = {}
        for b in reads:
            if b.w is not None:
                k, v = b.w
                if evs.get(k, 0) < v:
                    evs[k] = v
            if b.bank is not None:
                for k, v in b.r.items():
                    if k != ("e", eng) and evs.get(k, 0) < v:
                        evs[k] = v
        for b in writes:
            if b.w is not None:
                k, v = b.w
                if evs.get(k, 0) < v:
                    evs[k] = v
            for k, v in b.r.items():
                if evs.get(k, 0) < v:
                    evs[k] = v
        e = self.eng[eng]
        seen = self.seen[eng]
        for k, v in evs.items():
            if k == ("e", eng) and not self_sync:
                continue
            if eng == "pe" and k[0] == "p" and not self_sync:
                continue
            if seen.get(k, 0) < v:
                e.wait_ge(self.sem[k], v)
                seen[k] = v
                self.ninst += 1

    def _mark(self, ev, reads, writes):
        k, v = ev
        for b in reads:
            if b.r.get(k, 0) < v:
                b.r[k] = v
        for b in writes:
            b.w = ev
            b.r = {}

    def op(self, eng, fn, reads=(), writes=(), self_sync=True):
        reads = [t.buf for t in reads if isinstance(t, T)]
        writes = [t.buf for t in writes if isinstance(t, T)]
        self._waits(eng, reads, writes, self_sync)
        inst = fn(self.eng[eng])
        self.cnt[eng] += 1
        self.ninst += 1
        if eng == "pe":
            bank = writes[0].bank
            self.pe_cnt[bank] += 1
            key = ("p", bank)
            inst.then_inc(self.sem[key], 1)
            self._mark((key, self.pe_cnt[bank]), reads, writes)
        else:
            inst.then_inc(self.sem[("e", eng)], 1)
            self._mark((("e", eng), self.cnt[eng]), reads, writes)

    def dma(self, q, out, in_, is_output=False, indirect=None, **kw):
        reads = [t.buf for t in (in_, indirect) if isinstance(t, T)]
        writes = [t.buf for t in (out,) if isinstance(t, T)]
        dq = self.dq[q]
        si = dq["i"] % self.NDMA
        dq["i"] += 1
        key = dq["slots"][si]
        prev = dq["uses"][si] * 16
        seen = self.seen[q]
        self._waits(q, reads, writes, True)
        if prev and seen.get(key, 0) < prev:
            self.eng[q].wait_ge(self.sem[key], prev)
            seen[key] = prev
        if indirect is not None:
            inst = self.nc.gpsimd.indirect_dma_start(
                out=_ap(out), out_offset=None, in_=_ap(in_),
                in_offset=bass.IndirectOffsetOnAxis(ap=_ap(indirect), axis=0), **kw)
        else:
            inst = self.eng[q].dma_start(out=_ap(out), in_=_ap(in_), **kw)
        inst.then_inc(self.sem[key], 16)
        dq["uses"][si] += 1
        ev = (key, dq["uses"][si] * 16)
        self.ninst += 1
        self._mark(ev, reads, writes)
        if is_output:
            self.out_events.append(ev)

    def _all_events(self):
        evs = {}
        for e, c in self.cnt.items():
            if c and e != "pe":
                evs[("e", e)] = c
        for b, c in enumerate(self.pe_cnt):
            if c:
                evs[("p", b)] = c
        for q, dq in self.dq.items():
            for key, u in zip(dq["slots"], dq["uses"]):
                if u:
                    evs[key] = u * 16
        return evs

    def barrier(self):
        evs = self._all_events()
        sp = self.eng["sp"]
        seen = self.seen["sp"]
        for k, v in evs.items():
            if k == ("e", "sp"):
                continue
            if seen.get(k, 0) < v:
                sp.wait_ge(self.sem[k], v)
                seen[k] = v
        inst = sp.nop()
        self.cnt["sp"] += 1
        inst.then_inc(self.sem[("e", "sp")], 1)
        v = self.cnt["sp"]
        for e in self.eng:
            if e == "sp":
                continue
            self.eng[e].wait_ge(self.sem[("e", "sp")], v)
            self.seen[e][("e", "sp")] = v
            for k, vv in evs.items():
                if self.seen[e].get(k, 0) < vv:
                    self.seen[e][k] = vv

    def finish(self):
        evs = self._all_events()
        sp = self.eng["sp"]
        for k, v in evs.items():
            if k == ("e", "sp"):
                continue
            if self.seen["sp"].get(k, 0) < v:
                sp.wait_ge(self.sem[k], v)
                self.seen["sp"][k] = v

    def tt(self, eng, out, in0, in1, op):
        self.op(eng, lambda e: e.tensor_tensor(out=out.ap, in0=in0.ap, in1=in1.ap, op=op),
                reads=[in0, in1], writes=[out])

    def ts(self, eng, out, in0, s1, op0, s2=None, op1=None):
        if op1 is None:
            self.op(eng, lambda e: e.tensor_scalar(out=out.ap, in0=in0.ap, scalar1=_ap(s1), scalar2=None, op0=op0),
                    reads=[in0, s1], writes=[out])
        else:
            self.op(eng, lambda e: e.tensor_scalar(out=out.ap, in0=in0.ap, scalar1=_ap(s1), scalar2=_ap(s2),
                                                   op0=op0, op1=op1),
                    reads=[in0, s1, s2], writes=[out])

    def stt(self, out, in0, scalar, in1, op0, op1, accum=None):
        if accum is None:
            self.op("dve", lambda e: e.scalar_tensor_tensor(out=out.ap, in0=in0.ap, scalar=_ap(scalar), in1=in1.ap,
                                                            op0=op0, op1=op1),
                    reads=[in0, scalar, in1], writes=[out])
        else:
            self.op("dve", lambda e: e.scalar_tensor_tensor(out=out.ap, in0=in0.ap, scalar=_ap(scalar), in1=in1.ap,
                                                            op0=op0, op1=op1, accum_out=accum.ap),
                    reads=[in0, scalar, in1, accum], writes=[out, accum])

    def act(self, out, in_, func, bias=None, scale=1.0, accum=None):
        kw = {}
        if bias is not None:
            kw["bias"] = _ap(bias)
        if accum is not None:
            kw["accum_out"] = accum.ap
        self.op("act", lambda e: e.activation(out=out.ap, in_=in_.ap, func=func, scale=_ap(scale), **kw),
                reads=[in_, bias, scale, accum], writes=[out, accum] if accum is not None else [out])

    def copy(self, eng, out, in_):
        if eng == "act":
            self.op("act", lambda e: e.copy(out=out.ap, in_=in_.ap), reads=[in_], writes=[out])
        else:
            self.op(eng, lambda e: e.tensor_copy(out=out.ap, in_=in_.ap), reads=[in_], writes=[out])

    def memset(self, eng, out, val):
        self.op(eng, lambda e: e.memset(out.ap, val), writes=[out])

    def red(self, eng, out, in_, op=ALU.add):
        self.op(eng, lambda e: e.tensor_reduce(out=out.ap, in_=in_.ap, axis=AX.X, op=op), reads=[in_], writes=[out])

    def recip(self, out, in_):
        self.op("dve", lambda e: e.reciprocal(out=out.ap, in_=in_.ap), reads=[in_], writes=[out])

    def _pe_rowgroup(self, out, lhsT, is_tr=False):
        def rnd(n):
            return 32 if n <= 32 else (64 if n <= 64 else 128)
        k = rnd(lhsT.ap.partition_size())
        m = rnd(out.ap.partition_size())
        sig = (k, lhsT.ap.base_partition() if k < 128 else 0, m, out.ap.base_partition() if m < 128 else 0,
               is_tr, str(lhsT.ap.dtype))
        last = getattr(self, "_last_pe_sig", None)
        if last is not None and (getattr(self, "pe_serial", False) or (last != sig and not DEBUG.get("no_drain", NO_DRAIN))):
            for b in range(8):
                key, v = ("p", b), self.pe_cnt[b]
                if v and self.seen["pe"].get(key, 0) < v:
                    self.eng["pe"].wait_ge(self.sem[key], v)
                    self.seen["pe"][key] = v
                    self.ninst += 1
        self._last_pe_sig = sig

    def mm(self, out, lhsT, rhs, start=True, stop=True):
        self._pe_rowgroup(out, lhsT)
        self.op("pe", lambda e: e.matmul(out.ap, lhsT=lhsT.ap, rhs=rhs.ap, start=start, stop=stop),
                reads=[lhsT, rhs], writes=[out], self_sync=False)

    def tr(self, out, in_, ident):
        self._pe_rowgroup(out, in_, True)
        self.op("pe", lambda e: e.transpose(out=out.ap, in_=in_.ap, identity=ident.ap),
                reads=[in_, ident], writes=[out], self_sync=False)


class Carver:
    def __init__(self, base_ap, ncols):
        self.base = base_ap
        self.ncols = ncols
        self.off = 0

    def tile(self, parts, shape, dt=F32):
        n = int(np.prod(shape))
        assert self.off + n <= self.ncols, ("work region overflow", self.off, n, self.ncols)
        ap = self.base[0:parts, self.off:self.off + n]
        self.off += n
        if dt != F32:
            ap = ap.bitcast(dt)
        if len(shape) == 2:
            ap = ap.rearrange("p (a b) -> p a b", a=shape[0])
        elif len(shape) == 3:
            ap = ap.rearrange("p (a b c) -> p a b c", a=shape[0], b=shape[1])
        return T(ap)


def host_constants():
    c = {}
    c["ident"] = np.eye(128, dtype=np.float32)
    c["ones"] = np.ones((128, 128), np.float32)
    c["negones"] = -np.ones((128, 128), np.float32)
    bo = np.zeros((128, 128), np.float32)
    bo[:64, :64] = 1
    bo[64:, 64:] = 1
    c["blockones"] = bo
    i = np.arange(64)
    c["ltriT"] = (i[:, None] <= i[None, :]).astype(np.float32)
    c["negmask"] = np.where(i[None, :] <= i[:, None], 0.0, NEG).astype(np.float32)
    c["negstrict"] = np.where(i[None, :] < i[:, None], -1.0, 0.0).astype(np.float32)
    pm = np.zeros((128, 128), np.float32)
    for m in range(128):
        blk, r = divmod(m, 64)
        src = blk * 64 + (r + 32) % 64
        pm[src, m] = 1
    c["perm"] = pm
    half = 32
    inv = (1.0 / (10000.0 ** (np.arange(half, dtype=np.float32) / np.float32(half)))).astype(np.float32)
    pos = np.concatenate([np.arange(SEQ, dtype=np.float32), np.array([16384.0], np.float32)])
    ang = (pos[:, None] * inv[None, :]).astype(np.float32)
    cos = np.cos(ang).astype(np.float32).T
    sin = np.sin(ang).astype(np.float32).T
    C = np.concatenate([cos, cos, cos, cos], 0)
    S = np.concatenate([-sin, sin, -sin, sin], 0)
    c["ropeC"] = np.ascontiguousarray(C)
    c["ropeS"] = np.ascontiguousarray(S)
    lg = np.log1p(-np.exp2(-5.0 - np.arange(4, dtype=np.float32))).astype(np.float32)
    dmT = np.zeros((2, 64, 4, 64), np.float32)
    eGrow = np.zeros((2, 128, 2, 64), np.float32)
    eGr = np.zeros((2, 64, 4), np.float32)
    dl = np.zeros((2, 128, 4), np.float32)
    for h in range(4):
        d = (i[None, :] - i[:, None]).astype(np.float32)
        dmT[0, :, h, :] = np.where(d >= 0, np.exp(np.minimum(d, 64) * lg[h]), 0.0)
        dmT[1, :, h, :] = 0.0
        dmT[1, 0, h, 0] = 1.0
        eGr[0, :, h] = np.exp((63 - i).astype(np.float32) * lg[h])
        eGr[1, :, h] = 1.0
        dl[0, :, h] = np.exp(np.float32(64.0) * lg[h])
        dl[1, :, h] = np.exp(lg[h])
        k, pb = divmod(h, 2)
        eGrow[0, pb * 64:(pb + 1) * 64, k, :] = np.exp((i + 1).astype(np.float32) * lg[h])[None, :]
        eGrow[1, pb * 64:(pb + 1) * 64, k, :] = np.exp(lg[h])
    c["ret_dmT"] = dmT
    c["ret_eGrow"] = eGrow
    c["ret_eGr"] = eGr
    c["ret_dl"] = dl
    wins = [2, 4, 8, 16]
    pinv = np.zeros((2, 128, 2, 64), np.float32)
    t = np.arange(64)
    for k in range(2):
        for pbk in range(2):
            w = wins[2 * k + pbk]
            pinv[0, pbk * 64:(pbk + 1) * 64, k, :] = (1.0 / np.minimum(w, t + 1).astype(np.float32))[None, :]
            pinv[1, pbk * 64:(pbk + 1) * 64, k, :] = np.float32(1.0 / w)
    c["pool_inv"] = pinv
    c["iota16"] = np.broadcast_to(np.arange(16, dtype=np.float32)[None, :], (128, 16)).copy()
    return c


CONST_SHAPES = None


def build_program():
    nc = bass.Bass("TRN2", target_bir_lowering=False)
    kb = KB(nc)
    D = {}

    def din(name, shape, dt=F32):
        D[name] = nc.dram_tensor(name, list(shape), dt, kind="ExternalInput").ap()

    def dout(name, shape):
        D[name] = nc.dram_tensor(name, list(shape), F32, kind="ExternalOutput").ap()

    din("xp", [SEQ, D_MODEL])
    din("xs", [NS, D_MODEL])
    din("st_delta", [DEPTH, NS, 4, 64, 64])
    din("st_dconv", [DEPTH, NS, 3, 768])
    din("st_pool", [DEPTH, NS, 15, 256])
    din("st_ret", [DEPTH, NS, 4, 64, 64])
    din("st_conv", [DEPTH, NS, 30, 256])
    din("norm1", [DEPTH, D_MODEL])
    din("norm2", [DEPTH, D_MODEL])
    din("final_norm", [1, D_MODEL])
    din("w_in", [DEPTH, D_MODEL, IN_W])
    din("w_out", [DEPTH, D_MODEL, D_MODEL])
    din("peer_wq", [DEPTH, D_MODEL, 2048])
    din("keysT", [DEPTH, 16, 128, 128])
    din("peer_u", [DEPTH * 16384, D_MODEL])
    din("peer_v", [DEPTH * 16384, D_MODEL])
    din("dconv_wT", [DEPTH, 768, 4])
    din("dw_wT", [DEPTH, 256, 31])
    din("colvecs", [DEPTH, 128, 8])
    din("pw_w", [DEPTH, 256, 256])
    din("pool_w", [DEPTH, 4, 64, 64])
    din("a_log", [DEPTH, 4])
    din("dt_bias", [DEPTH, 4])
    din("dnorm_w", [DEPTH, 64])
    consts = host_constants()
    for k, v in consts.items():
        din("c_" + k, v.shape)

    dout("y_p", [SEQ, D_MODEL])
    dout("y_s", [NS, D_MODEL])
    dout("d_p", [DEPTH, 4, 64, 64])
    dout("dc_p", [DEPTH, 3, 768])
    dout("pl_p", [DEPTH, 15, 256])
    dout("r_p", [DEPTH, 4, 64, 64])
    dout("c_p", [DEPTH, 30, 256])
    dout("d_s", [DEPTH, NS, 4, 64, 64])
    dout("dc_s", [DEPTH, NS, 3, 768])
    dout("pl_s", [DEPTH, NS, 15, 256])
    dout("r_s", [DEPTH, NS, 4, 64, 64])
    dout("c_s", [DEPTH, NS, 30, 256])
    NTOK = SEQ + NS
    xmid = nc.dram_tensor("xmid", [NTOK, D_MODEL], F32, kind="Internal").ap()
    xnext = nc.dram_tensor("xnext", [NTOK, D_MODEL], F32, kind="Internal").ap()

    WCOLS = 30784
    wbuf = nc.alloc_sbuf_tensor("wbuf", [128, WCOLS], F32)
    bW = Buf()
    W_in = T(wbuf[:, 0:8 * IN_W].rearrange("p (k e) -> p k e", k=8), bW)
    W_out = T(wbuf[:, 8 * IN_W:8 * IN_W + 8192].rearrange("p (k e) -> p k e", k=8), bW)
    wq = T(wbuf[:, 0:16384].rearrange("p (k e) -> p k e", k=8), bW)
    keysT = T(wbuf[:, 16384:18432].rearrange("p (g n) -> p g n", g=16), bW)
    NTB = SEQ // 128 + 1

    def sb(name, shape, dt=F32):
        return T(nc.alloc_sbuf_tensor("sb_" + name, list(shape), dt)[:])

    C = {}
    for k, v in consts.items():
        if v.ndim == 2:
            C[k] = sb("sc_" + k, v.shape)
        else:
            shp = list(v.shape)
            C[k] = sb("sc_" + k, [shp[1], shp[0]] + shp[2:])
    eps_t = sb("eps_t", [128, 1])
    dconv_w = sb("dconv_w", [128, 6, 4])
    dw_w = sb("dw_w", [128, 2, 31])
    colv = sb("colv", [128, 8])
    pw_w = sb("pw_w", [128, 2, 256])
    pool_bd = sb("pool_bd", [128, 2, 128])
    alog = sb("alog", [64, 4])
    dtb = sb("dtb", [64, 4])
    negA = sb("negA", [64, 4])
    dnorm = sb("dnorm", [64, 64])
    WORKC = 14440
    work = nc.alloc_sbuf_tensor("work", [128, WORKC], F32)

    psum_raw = [nc.alloc_psum_tensor("ps%d" % i, [128, 512], F32) for i in range(8)]
    BK = [T(psum_raw[i][:]) for i in range(8)]
    for i_ in range(8):
        BK[i_].buf.bank = i_

    def BKV(bank, col0, ncols):
        assert col0 + ncols <= 512
        return BK[bank][:, col0:col0 + ncols]

    def PR(t):
        pass

    for k, v in consts.items():
        if k in ("ropeC", "ropeS"):
            continue
        if v.ndim == 2:
            kb.dma("sp", C[k], D["c_" + k])
        else:
            for kind in range(v.shape[0]):
                kb.dma("sp", C[k][:, kind], D["c_" + k][kind])
    kb.memset("pool", eps_t, EPS)
    ident = C["ident"]
    ones = C["ones"]

    def interleave(gens):
        gens = list(gens)
        if DEBUG.get("seq_heads"):
            for g in gens:
                for _ in g:
                    pass
            yield
            return
        while gens:
            nxt = []
            for g in gens:
                try:
                    next(g)
                    nxt.append(g)
                except StopIteration:
                    pass
            gens = nxt
            yield

    def run_lanes(gens):
        if DEBUG.get("seq_lanes"):
            for g in gens:
                for _ in g:
                    pass
            return
        for _ in interleave(gens):
            pass

    def phaseA(l, xin_p, xin_s):
        ca = Carver(work, WORKC)
        NWI = 8 * IN_W // 2
        W_in = T(wbuf[:, 0:NWI].bitcast(BF16).rearrange("p (k e) -> p k e", k=8), bW)
        W_out = T(wbuf[:, NWI:NWI + 4096].bitcast(BF16).rearrange("p (k e) -> p k e", k=8), bW)
        wa = Carver(wbuf, WCOLS)
        wa.off = NWI + 4096
        stg = [wa.tile(128, [IN_W]) for _ in range(2)]
        cast_eng = ["act", "pool", "dve"]
        for kc in range(8):
            st = stg[kc % 2]
            kb.dma("sp", st, D["w_in"][l, kc * 128:(kc + 1) * 128, :])
            kb.copy(cast_eng[kc % 3], W_in[:, kc, :], st)
        for kc in range(8):
            st = stg[kc % 2]
            kb.dma("sp", st[:, 0:1024], D["w_out"][l, kc * 128:(kc + 1) * 128, :])
            kb.copy(cast_eng[kc % 3], W_out[:, kc, :], st[:, 0:1024])
        n1bc = ca.tile(64, [1024])
        kb.dma("sp", n1bc, D["norm1"][l:l + 1, :].to_broadcast([64, 1024]))
        kb.dma("sp", dconv_w, D["dconv_wT"][l].rearrange("(k p) j -> p k j", p=128))
        kb.dma("sp", dw_w, D["dw_wT"][l].rearrange("(k p) j -> p k j", p=128))
        kb.dma("sp", colv, D["colvecs"][l])
        kb.dma("sp", pw_w, D["pw_w"][l].rearrange("(k p) d -> p k d", p=128))
        kb.memset("pool", pool_bd, 0.0)
        for g in range(4):
            k, pbk = divmod(g, 2)
            kb.dma("sp", pool_bd[pbk * 64:(pbk + 1) * 64, k, pbk * 64:(pbk + 1) * 64], D["pool_w"][l, g])
        kb.dma("sp", alog, D["a_log"][l:l + 1, :].to_broadcast([64, 4]))
        kb.dma("sp", dtb, D["dt_bias"][l:l + 1, :].to_broadcast([64, 4]))
        kb.dma("sp", dnorm, D["dnorm_w"][l:l + 1, :].to_broadcast([64, 64]))
        kb.act(negA, alog, AF.Exp)
        kb.ts("dve", negA, negA, -1.0, ALU.mult)
        dw_b, ln_w, ln_b, pscale = colv[:, 0:2], colv[:, 2:4], colv[:, 4:6], colv[:, 6:8]

        xA = ca.tile(64, [1024])
        hA = ca.tile(64, [1024])
        xo = ca.tile(64, [1024])
        ss = ca.tile(64, [4])
        hT = T(ca.tile(128, [8 * 32]).ap.bitcast(BF16).rearrange("p (k t) -> p k t", k=8))
        full_d = ca.tile(128, [6, 67])
        full_p = ca.tile(128, [2, 79])
        full_c = ca.tile(128, [2, 94])
        zT2 = ca.tile(128, [8, 64])
        ztok = ca.tile(64, [776])
        qkvc = ca.tile(128, [6, 64])
        qkvs = ca.tile(128, [6, 64])
        sq4 = ca.tile(128, [4, 64])
        rinv = ca.tile(128, [4, 64])
        qkn = ca.tile(128, [4, 64])
        kv_tok = ca.tile(64, [512])
        ab = ca.tile(64, [16, 4])
        GL = ca.tile(128, [4])
        dl = ca.tile(128, [4])
        Sd = ca.tile(128, [2, 64])
        Sr = ca.tile(128, [2, 64])
        HB = []
        for h in range(4):
            d = {}
            for nm in ("diagG", "dmm", "dm", "Araw", "Nm0", "Nm1", "Nt0", "Nt1", "Aqk", "AqkT", "Tt", "t1",
                       "rhsv", "vnew", "QSs", "Kd", "rAqkT", "rQSs"):
                d[nm] = wa.tile(64, [64])
            HB.append(d)
        o_tok = ca.tile(64, [256])
        osq = ca.tile(64, [256])
        ost = ca.tile(64, [8])
        ogate = ca.tile(64, [256])
        rsq = ca.tile(64, [256])
        rst = ca.tile(64, [8])
        rgate = ca.tile(64, [256])
        mixT = T(ca.tile(128, [8 * 32]).ap.bitcast(BF16).rearrange("p (k t) -> p k t", k=8))
        ps2 = ca.tile(128, [2, 79])
        ps4 = ca.tile(128, [2, 79])
        ps8 = ca.tile(128, [79])
        ps16 = ca.tile(128, [79])
        pw = ca.tile(128, [2, 64])
        ropeC = ca.tile(128, [64])
        ropeS = ca.tile(128, [64])
        ropeCs = [ropeC, ca.tile(128, [64])]
        ropeSs = [ropeS, ca.tile(128, [64])]
        rot1 = ca.tile(128, [4, 64])
        rot = ca.tile(128, [4, 64])
        QdT = ca.tile(128, [2, 64])
        Kd_tok = ca.tile(64, [256])
        r_tok = ca.tile(64, [256])
        rcen = ca.tile(64, [256])
        sg = ca.tile(128, [2, 64])
        cacc = ca.tile(128, [2, 64])
        ccen = ca.tile(128, [2, 64])
        csq = ca.tile(128, [2, 64])
        crs = ca.tile(128, [64])
        chn = ca.tile(128, [2, 64])
        hstage = [wa.tile(32, [768]), wa.tile(32, [256]), wa.tile(32, [256])]
        xAs = [xA, wa.tile(64, [1024])]
        zT1s = [wa.tile(128, [8, 64]) for _ in range(2)]
        zT2s = [zT2, wa.tile(128, [8, 64])]
        ztoks = [ztok, wa.tile(64, [776])]
        hload = xo

        def hist_in(full, nk, H, src):
            kb.dma("sp", hload[:H, 0:nk * 128], src)
            ps = BKV(6, 0, nk * H)
            for k in range(nk):
                kb.tr(ps[:, k * H:(k + 1) * H], hload[:H, k * 128:(k + 1) * 128], ident[:H, :H])
            kb.copy("act", full[:, :, 0:H], ps[:, 0:nk * H].r("p (k h) -> p k h", k=nk))
            PR(ps)

        def hist_out(full, nk, H, NT, dst, stage):
            for k in range(nk):
                ps = BKV(6, 384, 128)
                kb.tr(ps[:H, 0:128], full[:, k, NT:NT + H], ident)
                kb.copy("act", stage[:H, k * 128:(k + 1) * 128], ps[:H, 0:128])
            kb.dma("sp", dst, stage[:H, 0:nk * 128], is_output=True)

        def front_gen(kind, idx, par):
            NT = 64 if kind == "p" else 1
            row0 = idx * 64 if kind == "p" else SEQ + idx
            pcol = idx * 64 if kind == "p" else SEQ
            rkw = dict(allow_slow_non_contiguous=True) if NT == 1 else {}
            xA, zT1, zT2, ztok, ropeC, ropeS = xAs[par], zT1s[par], zT2s[par], ztoks[par], ropeCs[par], ropeSs[par]
            kb.dma("sp", xA[:NT], xin_p[row0:row0 + NT, :] if kind == "p" else xin_s[idx:idx + 1, :])
            kb.dma("sp", ropeC[:, :NT], D["c_ropeC"][:, pcol:pcol + NT], **rkw)
            kb.dma("sp", ropeS[:, :NT], D["c_ropeS"][:, pcol:pcol + NT], **rkw)
            kb.memset("pool", ss[:NT], 0.0)
            yield
            kb.act(hA[:NT], xA[:NT], AF.Square, accum=ss[:NT, 0:1])
            yield
            kb.act(ss[:NT, 1:2], ss[:NT, 0:1], AF.Sqrt, bias=eps_t[:NT], scale=1.0 / D_MODEL)
            yield
            kb.recip(ss[:NT, 2:3], ss[:NT, 1:2])
            yield
            kb.stt(hA[:NT], xA[:NT], ss[:NT, 2:3], n1bc[:NT], ALU.mult, ALU.mult)
            yield
            ps = BKV(7, 0, 512)
            for kc in range(8):
                kb.tr(ps[:, kc * NT:(kc + 1) * NT], hA[:NT, kc * 128:(kc + 1) * 128], ident[:NT, :NT])
            yield
            kb.copy("act", hT[:, :, :NT], ps[:, 0:8 * NT].r("p (k t) -> p k t", k=8))
            yield

            def zfm(ps, slot, col0):
                for kc in range(8):
                    kb.mm(ps[:, slot * NT:(slot + 1) * NT], W_in[:, kc, col0:col0 + 128], hT[:, kc, :NT],
                          start=(kc == 0), stop=(kc == 7))
            ps = BKV(7, 0, 512)
            for k in range(6):
                zfm(ps, k, k * 128)
                if k % 2 == 1:
                    yield
            for k in range(2):
                zfm(ps, 6 + k, 1032 + k * 128)
            yield
            kb.copy("act", zT1[:, :, :NT], ps[:, 0:8 * NT].r("p (k t) -> p k t", k=8))
            yield
            ps = BKV(7, 0, 512)
            for k in range(4):
                zfm(ps, k, 1288 + k * 128)
                if k % 2 == 1:
                    yield
            for k in range(4):
                zfm(ps, 4 + k, 2312 + k * 128)
                if k % 2 == 1:
                    yield
            kb.copy("act", zT2[:, :, :NT], ps[:, 0:8 * NT].r("p (k t) -> p k t", k=8))
            yield
            ps = BKV(7, 0, 512)
            for kc in range(8):
                kb.mm(ps[:NT, 0:264], hT[:, kc, :NT], W_in[:, kc, 768:1032], start=(kc == 0), stop=(kc == 7))
            yield
            kb.copy("act", ztok[:NT, 0:264], ps[:NT, 0:264])
            yield
            ps = BKV(7, 0, 512)
            for kc in range(8):
                kb.mm(ps[:NT, 0:512], hT[:, kc, :NT], W_in[:, kc, 1800:2312], start=(kc == 0), stop=(kc == 7))
            yield
            kb.copy("act", ztok[:NT, 264:776], ps[:NT, 0:512])

        def tile(kind, idx, par, nxt_front):
            NT = 64 if kind == "p" else 1
            kk = 0 if kind == "p" else 1
            first = kind == "p" and idx == 0
            last = (kind == "s") or idx == SEQ // 64 - 1
            row0 = idx * 64 if kind == "p" else SEQ + idx
            xA, zT1, zT2, ztok, ropeC, ropeS = xAs[par], zT1s[par], zT2s[par], ztoks[par], ropeCs[par], ropeSs[par]
            if first:
                kb.memset("pool", full_d[:, :, 0:3], 0.0)
                kb.memset("pool", full_p[:, :, 0:15], 0.0)
                kb.memset("pool", full_c[:, :, 0:30], 0.0)
                kb.memset("pool", Sd, 0.0)
                kb.memset("pool", Sr, 0.0)
            elif kind == "p":
                kb.copy("pool", full_d[:, :, 0:3], full_d[:, :, 64:67])
                kb.copy("pool", full_p[:, :, 0:15], full_p[:, :, 64:79])
                kb.copy("pool", full_c[:, :, 0:30], full_c[:, :, 64:94])
            else:
                hist_in(full_d, 6, 3, D["st_dconv"][l, idx])
                hist_in(full_p, 2, 15, D["st_pool"][l, idx])
                hist_in(full_c, 2, 30, D["st_conv"][l, idx])
                for pb_ in range(2):
                    pbs_ = slice(pb_ * 64, (pb_ + 1) * 64)
                    kb.dma("sp", Sd[pbs_], D["st_delta"][l, idx].rearrange("(q b) d e -> b d q e", b=2)[pb_])
                    kb.dma("sp", Sr[pbs_], D["st_ret"][l, idx].rearrange("(q b) d e -> b d q e", b=2)[pb_])
            kb.copy("act", full_d[:, :, 3:3 + NT], zT1[:, 0:6, :NT])
            kb.copy("pool", full_p[:, :, 15:15 + NT], zT1[:, 6:8, :NT])

            a_t, b_t = ztok[:NT, 256:260], ztok[:NT, 260:264]
            beta, nbeta, xsp, axs, esp, lsp, gg, Gs, eG, eGr, tmpg = [ab[:NT, i, :] for i in range(11)]

            def delta_head(h):
                B = HB[h]
                qc, pb = divmod(h, 2)
                pbs = slice(pb * 64, (pb + 1) * 64)
                qT_h = qkn[pbs, qc, :NT]
                kT_h = qkn[pbs, 2 + qc, :NT]
                k_tok_h = kv_tok[:NT, h * 64:(h + 1) * 64]
                v_tok_h = kv_tok[:NT, 256 + h * 64:256 + (h + 1) * 64]
                S_h = Sd[pbs, qc, :]
                N = lambda t: t[:NT, :NT]
                psC = BKV(h, 0, 128)
                kb.mm(psC[:NT, 0:64], kT_h, S_h)
                kb.mm(psC[:NT, 64:128], qT_h, S_h)
                yield
                kb.stt(B["t1"][:NT], psC[:NT, 0:64], eG[:, h:h + 1], v_tok_h, ALU.mult, ALU.subtract)
                kb.act(B["QSs"][:NT], psC[:NT, 64:128], AF.Copy, scale=eG[:, h:h + 1])
                PR(psC)
                kb.ts("dve", B["Kd"][:NT], k_tok_h, eGr[:, h:h + 1], ALU.mult)
                yield
                kb.ts("dve", B["rhsv"][:NT], B["t1"][:NT], nbeta[:, h:h + 1], ALU.mult)
                if NT > 1:
                    kb.ts("dve", N(B["diagG"]), ident[:NT, :NT], Gs[:, h:h + 1], ALU.mult)
                    yield
                    psB = BKV(h, 128, 64)
                    kb.mm(psB[:NT, 0:NT], N(B["diagG"]), ones[:NT, :NT], start=True, stop=False)
                    kb.mm(psB[:NT, 0:NT], C["negones"][:NT, :NT], N(B["diagG"]), start=False, stop=True)
                    yield
                    kb.tt("dve", N(B["dmm"]), psB[:NT, 0:NT], C["negmask"][:NT, :NT], ALU.add)
                    PR(psB)
                    yield
                    kb.act(N(B["dm"]), N(B["dmm"]), AF.Exp)
                    psA = BKV(h, 192, 128)
                    kb.mm(psA[:NT, 0:NT], kT_h, kT_h)
                    kb.mm(psA[:NT, 64:64 + NT], qT_h, kT_h)
                    yield
                    kb.stt(N(B["Araw"]), psA[:NT, 0:NT], beta[:, h:h + 1], N(B["dm"]), ALU.mult, ALU.mult)
                    kb.tt("dve", N(B["Aqk"]), psA[:NT, 64:64 + NT], N(B["dm"]), ALU.mult)
                    PR(psA)
                    yield
                    kb.tt("dve", N(B["Nm0"]), N(B["Araw"]), C["negstrict"][:NT, :NT], ALU.mult)
                    yield
                    psT = BKV(h, 320, 128)
                    kb.tr(psT[:NT, 0:NT], N(B["Nm0"]), ident[:NT, :NT])
                    kb.tr(psT[:NT, 64:64 + NT], N(B["Aqk"]), ident[:NT, :NT])
                    yield
                    kb.copy("act", N(B["Nt0"]), psT[:NT, 0:NT])
                    kb.copy("act", N(B["AqkT"]), psT[:NT, 64:64 + NT])
                    PR(psT)
                    yield
                    kb.tt("dve", N(B["Tt"]), ident[:NT, :NT], N(B["Nt0"]), ALU.add)
                    cur = 0
                    for lev in range(1, 6):
                        nxt = 1 - cur
                        Pc, Ptc, Pn, Ptn = B["Nm%d" % cur], B["Nt%d" % cur], B["Nm%d" % nxt], B["Nt%d" % nxt]
                        psP = BKV(h, 0, 128)
                        kb.mm(psP[:NT, 0:NT], N(Ptc), N(Pc))
                        if lev < 5:
                            kb.mm(psP[:NT, 64:64 + NT], N(Pc), N(Ptc))
                        yield
                        kb.copy("act", N(Pn), psP[:NT, 0:NT])
                        if lev < 5:
                            kb.copy("act", N(Ptn), psP[:NT, 64:64 + NT])
                        PR(psP)
                        yield
                        psU = BKV(h, 128, 64)
                        kb.mm(psU[:NT, 0:NT], N(Pn), N(B["Tt"]))
                        yield
                        kb.tt("dve", N(B["Tt"]), N(B["Tt"]), psU[:NT, 0:NT], ALU.add)
                        PR(psU)
                        yield
                        cur = nxt
                    Tt_use = N(B["Tt"])
                else:
                    psA = BKV(h, 192, 128)
                    kb.mm(psA[:NT, 64:64 + NT], qT_h, kT_h)
                    yield
                    kb.copy("act", N(B["AqkT"]), psA[:NT, 64:64 + NT])
                    PR(psA)
                    Tt_use = ident[:NT, :NT]
                    yield
                psD = BKV(h, 256, 128)
                kb.mm(psD[:NT, 0:64], Tt_use, B["rhsv"][:NT])
                yield
                kb.copy("act", B["vnew"][:NT], psD[:NT, 0:64])
                yield
                kb.mm(psD[:NT, 64:128], N(B["AqkT"]), B["vnew"][:NT])
                psE = BKV(h, 384, 64)
                kb.mm(psE[pbs, 0:64], B["Kd"][:NT], B["vnew"][:NT])
                yield
                kb.tt("dve", o_tok[:NT, h * 64:(h + 1) * 64], B["QSs"][:NT], psD[:NT, 64:128], ALU.add)
                kb.stt(S_h, S_h, dl[pbs, h:h + 1], psE[pbs, 0:64], ALU.mult, ALU.add)
                PR(psD)
                PR(psE)

            def delta_lane():
                for k in range(6):
                    kb.ts("dve", qkvc[:, k, :NT], full_d[:, k, 0:NT], dconv_w[:, k, 0:1], ALU.mult)
                    for j in range(1, 4):
                        kb.stt(qkvc[:, k, :NT], full_d[:, k, j:j + NT], dconv_w[:, k, j:j + 1], qkvc[:, k, :NT],
                               ALU.mult, ALU.add)
                    if k % 2 == 1:
                        yield
                kb.act(qkvs[:, :, :NT], qkvc[:, :, :NT], AF.Silu)
                kb.act(beta, b_t, AF.Sigmoid)
                kb.tt("dve", xsp, a_t, dtb[:NT], ALU.add)
                yield
                kb.tt("dve", sq4[:, :, :NT], qkvs[:, 0:4, :NT], qkvs[:, 0:4, :NT], ALU.mult)
                kb.ts("dve", nbeta, beta, -1.0, ALU.mult)
                kb.ts("dve", axs, xsp, -1.0, ALU.mult)
                kb.tt("dve", axs, axs, xsp, ALU.max)
                yield
                ps = BKV(0, 0, 256)
                for k in range(4):
                    kb.mm(ps[:, k * NT:(k + 1) * NT], C["blockones"], sq4[:, k, :NT])
                kb.act(esp, axs, AF.Exp, scale=-1.0)
                yield
                kb.act(rinv[:, :, :NT], ps[:, 0:4 * NT].r("p (k t) -> p k t", k=4), AF.Sqrt, bias=eps_t)
                PR(ps)
                kb.act(lsp, esp, AF.Ln, bias=1.0)
                kb.ts("dve", xsp, xsp, 0.0, ALU.max)
                yield
                kb.recip(rinv[:, :, :NT], rinv[:, :, :NT])
                kb.tt("dve", lsp, lsp, xsp, ALU.add)
                kb.tt("dve", gg, lsp, negA[:NT], ALU.mult)
                yield
                kb.stt(qkn[:, 0:2, :NT], qkvs[:, 0:2, :NT], 0.125, rinv[:, 0:2, :NT], ALU.mult, ALU.mult)
                kb.tt("dve", qkn[:, 2:4, :NT], qkvs[:, 2:4, :NT], rinv[:, 2:4, :NT], ALU.mult)
                ps2_ = BKV(1, 0, 128)
                kb.mm(ps2_[:NT, 0:4], C["ltriT"][:NT, :NT], gg)
                kb.mm(ps2_[:, 8:12], ones[:NT, :], gg)
                yield
                ps = BKV(2, 0, 512)
                for k in range(2):
                    kb.tr(ps[:NT, k * 128:(k + 1) * 128], qkn[:, 2 + k, :NT], ident)
                    kb.tr(ps[:NT, 256 + k * 128:256 + (k + 1) * 128], qkvs[:, 4 + k, :NT], ident)
                kb.copy("act", Gs, ps2_[:NT, 0:4])
                kb.copy("act", GL, ps2_[:, 8:12])
                PR(ps2_)
                yield
                kb.copy("act", kv_tok[:NT], ps[:NT, 0:512])
                PR(ps)
                kb.act(eG, Gs, AF.Exp)
                kb.act(dl, GL, AF.Exp)
                kb.tt("dve", tmpg, GL[:NT], Gs, ALU.subtract)
                yield
                kb.act(eGr, tmpg, AF.Exp)
                yield
                if DEBUG.get("pairs"):
                    yield from interleave([delta_head(h) for h in (0, 2)])
                    yield from interleave([delta_head(h) for h in (1, 3)])
                else:
                    yield from interleave([delta_head(h) for h in (0, 2, 1, 3)])
                o3 = o_tok[:NT].r("p (h e) -> p h e", h=4)
                q3 = osq[:NT].r("p (h e) -> p h e", h=4)
                kb.tt("dve", osq[:NT], o_tok[:NT], o_tok[:NT], ALU.mult)
                kb.act(ogate[:NT], ztok[:NT, 0:256], AF.Silu)
                yield
                kb.red("dve", ost[:NT, 0:4], q3)
                yield
                kb.act(ost[:NT, 4:8], ost[:NT, 0:4], AF.Sqrt, bias=eps_t[:NT], scale=1.0 / 64)
                yield
                kb.recip(ost[:NT, 4:8], ost[:NT, 4:8])
                kb.tt("dve", q3, o3, ost[:NT, 4:8].un(2).bc([NT, 4, 64]), ALU.mult)
                kb.tt("dve", q3, q3, dnorm[:NT].un(1).bc([NT, 4, 64]), ALU.mult)
                kb.tt("dve", osq[:NT], osq[:NT], ogate[:NT], ALU.mult)
                yield
                ps = BKV(0, 0, 128)
                for k in range(2):
                    kb.tr(ps[:, k * NT:(k + 1) * NT], osq[:NT, k * 128:(k + 1) * 128], ident[:NT, :NT])
                yield
                kb.copy("act", mixT[:, 0:2, :NT], ps[:, 0:2 * NT].r("p (k t) -> p k t", k=2))
                PR(ps)
                if last:
                    hist_out(full_d, 6, 3, NT, D["dc_p"][l] if kind == "p" else D["dc_s"][l, idx], hstage[0])
                    dst = D["d_p"][l] if kind == "p" else D["d_s"][l, idx]
                    for pb_ in range(2):
                        kb.dma("sp", dst.rearrange("(q b) d e -> b d q e", b=2)[pb_], Sd[pb_ * 64:(pb_ + 1) * 64],
                               is_output=True)

            def pool_lane():
                L = 15 + NT
                kb.tt("pool", ps2[:, :, 1:L], full_p[:, :, 1:L], full_p[:, :, 0:L - 1], ALU.add)
                yield
                kb.tt("pool", ps4[:, :, 3:L], ps2[:, :, 3:L], ps2[:, :, 1:L - 2], ALU.add)
                yield
                kb.tt("pool", ps8[:, 7:L], ps4[:, 1, 7:L], ps4[:, 1, 3:L - 4], ALU.add)
                yield
                kb.tt("pool", ps16[64:128, 15:L], ps8[64:128, 15:L], ps8[64:128, 7:L - 8], ALU.add)
                yield
                pinv = C["pool_inv"][:, 0 if first else 1]
                srcs = [(slice(0, 64), 0, ps2[0:64, 0, 15:L]), (slice(64, 128), 0, ps4[64:128, 0, 15:L]),
                        (slice(0, 64), 1, ps8[0:64, 15:L]), (slice(64, 128), 1, ps16[64:128, 15:L])]
                for (psl, k, src) in srcs:
                    kb.tt("pool", pw[psl, k, :NT], src, pinv[psl, k, :NT], ALU.mult)
                yield
                kb.tt("pool", pw[:, :, :NT], pw[:, :, :NT], full_p[:, :, 15:L], ALU.subtract)
                yield
                ps = BKV(6, 256, 128)
                for k in range(2):
                    kb.mm(ps[:, k * NT:(k + 1) * NT], pool_bd[:, k, :], pw[:, k, :NT])
                yield
                for k in range(2):
                    kb.ts("dve", mixT[:, 2 + k, :NT], ps[:, k * NT:(k + 1) * NT], pscale[:, k:k + 1], ALU.mult)
                PR(ps)
                if last:
                    hist_out(full_p, 2, 15, NT, D["pl_p"][l] if kind == "p" else D["pl_s"][l, idx], hstage[1])

            def ret_head(h):
                B = HB[h]
                qc, pb = divmod(h, 2)
                pbs = slice(pb * 64, (pb + 1) * 64)
                v_h = ztok[:NT, 264 + h * 64:264 + (h + 1) * 64]
                S_h = Sr[pbs, qc, :]
                psA = BKV(4 + pb, (h // 2) * 128, 128)
                kb.mm(psA[:NT, 0:NT], rot[pbs, 2 + qc, :NT], rot[pbs, qc, :NT])
                kb.mm(psA[:NT, 64:128], QdT[pbs, qc, :NT], S_h)
                psE = BKV(4, 256 + h * 64, 64)
                kb.mm(psE[pbs, 0:64], Kd_tok[:NT, h * 64:(h + 1) * 64], v_h)
                yield
                kb.tt("dve", B["rAqkT"][:NT, :NT], psA[:NT, 0:NT], C["ret_dmT"][:NT, kk, h, :NT], ALU.mult)
                kb.copy("act", B["rQSs"][:NT], psA[:NT, 64:128])
                kb.stt(S_h, S_h, C["ret_dl"][pbs, kk, h:h + 1], psE[pbs, 0:64], ALU.mult, ALU.add)
                PR(psA)
                PR(psE)
                yield
                psO = BKV(4, 256 + h * 64, 64)
                kb.mm(psO[:NT, 0:64], B["rAqkT"][:NT, :NT], v_h)
                yield
                kb.tt("dve", r_tok[:NT, h * 64:(h + 1) * 64], B["rQSs"][:NT], psO[:NT, 0:64], ALU.add)
                PR(psO)

            def ret_lane():
                ps = BKV(4, 0, 256)
                for k in range(4):
                    kb.mm(ps[:, k * NT:(k + 1) * NT], C["perm"], zT2[:, k, :NT])
                kb.tt("dve", rot1[:, :, :NT], zT2[:, 0:4, :NT], ropeC[:, :NT].un(1).bc([128, 4, NT]), ALU.mult)
                yield
                kb.tt("dve", rot[:, :, :NT], ps[:, 0:4 * NT].r("p (k t) -> p k t", k=4),
                      ropeS[:, :NT].un(1).bc([128, 4, NT]), ALU.mult)
                PR(ps)
                yield
                kb.tt("dve", rot[:, :, :NT], rot[:, :, :NT], rot1[:, :, :NT], ALU.add)
                yield
                kb.ts("dve", rot[:, 2:4, :NT], rot[:, 2:4, :NT], 0.125, ALU.mult)
                kb.tt("dve", QdT[:, :, :NT], rot[:, 0:2, :NT], C["ret_eGrow"][:, kk, :, :NT], ALU.mult)
                yield
                ps = BKV(5, 0, 256)
                for k in range(2):
                    kb.tr(ps[:NT, k * 128:(k + 1) * 128], rot[:, 2 + k, :NT], ident)
                yield
                kb.tt("dve", Kd_tok[:NT].r("p (h e) -> p h e", h=4), ps[:NT, 0:256].r("p (h e) -> p h e", h=4),
                      C["ret_eGr"][:NT, kk, :].un(2).bc([NT, 4, 64]), ALU.mult)
                PR(ps)
                yield
                yield from interleave([ret_head(h) for h in (0, 2, 1, 3)])
                r3 = r_tok[:NT].r("p (h e) -> p h e", h=4)
                c3 = rcen[:NT].r("p (h e) -> p h e", h=4)
                kb.red("dve", rst[:NT, 0:4], r3)
                kb.act(rgate[:NT], ztok[:NT, 520:776], AF.Silu)
                yield
                kb.ts("dve", rst[:NT, 0:4], rst[:NT, 0:4], 1.0 / 64, ALU.mult)
                kb.tt("dve", c3, r3, rst[:NT, 0:4].un(2).bc([NT, 4, 64]), ALU.subtract)
                kb.tt("dve", rsq[:NT], rcen[:NT], rcen[:NT], ALU.mult)
                kb.red("dve", rst[:NT, 0:4], rsq[:NT].r("p (h e) -> p h e", h=4))
                yield
                kb.act(rst[:NT, 4:8], rst[:NT, 0:4], AF.Sqrt, bias=eps_t[:NT], scale=1.0 / 64)
                yield
                kb.recip(rst[:NT, 4:8], rst[:NT, 4:8])
                kb.tt("dve", c3, c3, rst[:NT, 4:8].un(2).bc([NT, 4, 64]), ALU.mult)
                kb.tt("dve", rcen[:NT], rcen[:NT], rgate[:NT], ALU.mult)
                yield
                ps = BKV(4, 0, 128)
                for k in range(2):
                    kb.tr(ps[:, k * NT:(k + 1) * NT], rcen[:NT, k * 128:(k + 1) * 128], ident[:NT, :NT])
                yield
                kb.copy("act", mixT[:, 4:6, :NT], ps[:, 0:2 * NT].r("p (k t) -> p k t", k=2))
                PR(ps)
                if last:
                    dst = D["r_p"][l] if kind == "p" else D["r_s"][l, idx]
                    for pb_ in range(2):
                        kb.dma("sp", dst.rearrange("(q b) d e -> b d q e", b=2)[pb_], Sr[pb_ * 64:(pb_ + 1) * 64],
                               is_output=True)

            def conf_lane():
                kb.act(sg[:, :, :NT], zT2[:, 6:8, :NT], AF.Sigmoid)
                yield
                kb.tt("pool", full_c[:, :, 30:30 + NT], zT2[:, 4:6, :NT], sg[:, :, :NT], ALU.mult)
                yield
                for k in range(2):
                    kb.ts("dve", cacc[:, k, :NT], full_c[:, k, 0:NT], dw_w[:, k, 0:1], ALU.mult, dw_b[:, k:k + 1], ALU.add)
                for j in range(1, 31):
                    for k in range(2):
                        kb.stt(cacc[:, k, :NT], full_c[:, k, j:j + NT], dw_w[:, k, j:j + 1], cacc[:, k, :NT],
                               ALU.mult, ALU.add)
                    if j % 2 == 0:
                        yield
                ps = BKV(6, 0, 64)
                kb.mm(ps[:, 0:NT], ones, cacc[:, 0, :NT], start=True, stop=False)
                kb.mm(ps[:, 0:NT], ones, cacc[:, 1, :NT], start=False, stop=True)
                yield
                for k in range(2):
                    kb.stt(ccen[:, k, :NT], ps[:, 0:NT], -1.0 / 256, cacc[:, k, :NT], ALU.mult, ALU.add)
                PR(ps)
                yield
                kb.tt("pool", csq[:, :, :NT], ccen[:, :, :NT], ccen[:, :, :NT], ALU.mult)
                yield
                ps = BKV(6, 64, 64)
                kb.mm(ps[:, 0:NT], ones, csq[:, 0, :NT], start=True, stop=False)
                kb.mm(ps[:, 0:NT], ones, csq[:, 1, :NT], start=False, stop=True)
                yield
                kb.act(crs[:, :NT], ps[:, 0:NT], AF.Sqrt, bias=eps_t, scale=1.0 / 256)
                PR(ps)
                yield
                kb.recip(crs[:, :NT], crs[:, :NT])
                kb.tt("dve", chn[:, :, :NT], ccen[:, :, :NT], crs[:, :NT].un(1).bc([128, 2, NT]), ALU.mult)
                for k in range(2):
                    kb.ts("dve", chn[:, k, :NT], chn[:, k, :NT], ln_w[:, k:k + 1], ALU.mult, ln_b[:, k:k + 1], ALU.add)
                yield
                kb.act(csq[:, :, :NT], chn[:, :, :NT], AF.Silu)
                yield
                ps = BKV(6, 128, 128)
                for dk in range(2):
                    for ck in range(2):
                        kb.mm(ps[:, dk * NT:(dk + 1) * NT], pw_w[:, ck, dk * 128:(dk + 1) * 128], csq[:, ck, :NT],
                              start=(ck == 0), stop=(ck == 1))
                yield
                kb.copy("act", mixT[:, 6:8, :NT], ps[:, 0:2 * NT].r("p (k t) -> p k t", k=2))
                PR(ps)
                if last:
                    hist_out(full_c, 2, 30, NT, D["c_p"][l] if kind == "p" else D["c_s"][l, idx], hstage[2])

            def ntimes(g, n):
                while True:
                    for _ in range(n):
                        try:
                            next(g)
                        except StopIteration:
                            return
                    yield
            lanes = [ntimes(delta_lane(), DELTA_PRIO), ret_lane(), conf_lane(), pool_lane()]
            if nxt_front is not None:
                lanes.append(nxt_front)
            run_lanes(lanes)

            for nch in range(2):
                ps = BKV(6 + nch, 0, 512)
                for ec in range(8):
                    kb.mm(ps[:NT, 0:512], mixT[:, ec, :NT], W_out[:, ec, nch * 512:(nch + 1) * 512],
                          start=(ec == 0), stop=(ec == 7))
                kb.tt("dve", xo[:NT, nch * 512:(nch + 1) * 512], xA[:NT, nch * 512:(nch + 1) * 512],
                      ps[:NT, 0:512], ALU.add)
                PR(ps)
            kb.dma("sp", xmid[row0:row0 + NT, :], xo[:NT])

        seq_ = [("p", i) for i in range(DEBUG.get("n_ptiles", SEQ // 64))] + \
               [("s", s_) for s_ in range(DEBUG.get("n_stiles", NS))]
        if seq_:
            run_lanes([front_gen(seq_[0][0], seq_[0][1], 0)])
        for t_, (kind_, idx_) in enumerate(seq_):
            nf = front_gen(seq_[t_ + 1][0], seq_[t_ + 1][1], (t_ + 1) % 2) if t_ + 1 < len(seq_) else None
            tile(kind_, idx_, t_ % 2, nf)

    def phaseB(l, xdst, is_last):
        cb = Carver(work, WORKC)
        eidx_all = cb.tile(128, [NTB, 128], I32)
        gate_all = cb.tile(128, [NTB, 128])
        n2bc = cb.tile(128, [1024])
        fbc = cb.tile(128, [1024])
        mark = cb.off
        kb.dma("sp", wq, D["peer_wq"][l].rearrange("(k p) e -> p k e", p=128))
        kb.dma("sp", keysT, D["keysT"][l].rearrange("g c n -> c g n"))
        kb.dma("sp", n2bc, D["norm2"][l:l + 1, :].to_broadcast([128, 1024]))
        if is_last:
            kb.dma("sp", fbc, D["final_norm"][0:1, :].to_broadcast([128, 1024]))
        xB = cb.tile(128, [1024])
        h2 = cb.tile(128, [1024])
        ss = cb.tile(128, [4])
        h2T = cb.tile(128, [8, 128])
        qTg = [cb.tile(128, [128]) for _ in range(2)]
        swork = cb.tile(128, [128])
        vals = cb.tile(128, [16, 16])
        idxu = cb.tile(128, [16, 16], U32)
        idxf = cb.tile(128, [16, 16])
        cwork = cb.tile(128, [256])
        best = cb.tile(128, [8, 16])
        sel = cb.tile(128, [8, 16], U32)
        k1u = cb.tile(128, [128], U32)
        k2u = cb.tile(128, [128], U32)
        k1f = cb.tile(128, [128])
        k2f = cb.tile(128, [128])
        i1 = cb.tile(128, [128])
        i2 = cb.tile(128, [128])
        eidf = cb.tile(128, [128])
        gate = cb.tile(128, [8, 16])
        gsum = cb.tile(128, [8])
        tail = 18432
        s_sbs = [T(wbuf[:, tail:tail + 2048].rearrange("p (a b) -> p a b", a=16)),
                 T(wbuf[:, tail + 6144:tail + 8192].rearrange("p (a b) -> p a b", a=16))]
        cand = T(wbuf[:, tail + 2048:tail + 4096].rearrange("p (a b) -> p a b", a=8))
        oh = T(wbuf[:, tail + 4096:tail + 6144].rearrange("p (a b) -> p a b", a=128))

        def b1_front(ti, row0, NT):
            s_sb = s_sbs[ti % 2]
            kb.dma("sp", xB[:NT], xmid[row0:row0 + NT, :])
            kb.memset("pool", ss[:NT], 0.0)
            kb.act(h2[:NT], xB[:NT], AF.Square, accum=ss[:NT, 0:1])
            kb.act(ss[:NT, 1:2], ss[:NT, 0:1], AF.Sqrt, bias=eps_t[:NT], scale=1.0 / D_MODEL)
            kb.recip(ss[:NT, 2:3], ss[:NT, 1:2])
            kb.stt(h2[:NT], xB[:NT], ss[:NT, 2:3], n2bc[:NT], ALU.mult, ALU.mult)
            for half in range(2):
                ps = BKV(half, 0, 512)
                for kc in range(4):
                    kb.tr(ps[:, kc * NT:(kc + 1) * NT], h2[:NT, (half * 4 + kc) * 128:(half * 4 + kc + 1) * 128],
                          ident[:NT, :NT])
                kb.copy("act", h2T[:, half * 4:half * 4 + 4, :NT], ps[:, 0:4 * NT].r("p (k t) -> p k t", k=4))
                PR(ps)
                yield
            pss = None
            for g in range(16):
                psq = BKV(2 + g % 4, 0, 128)
                for kc in range(8):
                    kb.mm(psq[:, 0:NT], wq[:, kc, g * 128:(g + 1) * 128], h2T[:, kc, :NT],
                          start=(kc == 0), stop=(kc == 7))
                qt = qTg[g % 2]
                kb.copy("act", qt[:, :NT], psq[:, 0:NT])
                PR(psq)
                if g % 4 == 0:
                    pss = BKV(6 + (g // 4) % 2, 0, 512)
                kb.mm(pss[:NT, (g % 4) * 128:(g % 4 + 1) * 128], qt[:, :NT], keysT[:, g, :])
                if g % 4 == 3:
                    kb.copy("act", s_sb[:NT, g - 3:g + 1, :], pss[:NT, 0:512].r("p (g n) -> p g n", g=4))
                    PR(pss)
                yield

        def b1_topk(ti, row0, NT):
            s_sb = s_sbs[ti % 2]
            for g in range(16):
                kb.op("dve", lambda e, g=g: e.max(out=vals.ap[:NT, g, 0:8], in_=s_sb.ap[:NT, g, :]),
                      reads=[s_sb], writes=[vals])
                kb.op("dve", lambda e, g=g: e.max_index(out=idxu.ap[:NT, g, 0:8], in_max=vals.ap[:NT, g, 0:8],
                                                        in_values=s_sb.ap[:NT, g, :]),
                      reads=[s_sb, vals], writes=[idxu])
                kb.op("dve", lambda e, g=g: e.match_replace(out=swork.ap[:NT], in_to_replace=vals.ap[:NT, g, 0:8],
                                                            in_values=s_sb.ap[:NT, g, :], imm_value=NEG),
                      reads=[s_sb, vals], writes=[swork])
                kb.op("dve", lambda e, g=g: e.max(out=vals.ap[:NT, g, 8:16], in_=swork.ap[:NT]),
                      reads=[swork], writes=[vals])
                kb.op("dve", lambda e, g=g: e.max_index(out=idxu.ap[:NT, g, 8:16], in_max=vals.ap[:NT, g, 8:16],
                                                        in_values=swork.ap[:NT]),
                      reads=[swork, vals], writes=[idxu])
                yield
            kb.copy("dve", idxf[:NT], idxu[:NT])
            v4 = vals[:NT].r("p (h two) k -> p h two k", two=2)
            i4 = idxf[:NT].r("p (h two) k -> p h two k", two=2)
            cand4 = cand[:NT].r("p h (a b) -> p h a b", a=16)
            for h in range(8):
                kb.tt("dve", cand4[:, h], v4[:, h, 0, :].un(2).bc([NT, 16, 16]), v4[:, h, 1, :].un(1).bc([NT, 16, 16]),
                      ALU.add)
            for h in range(8):
                kb.op("dve", lambda e, h=h: e.max(out=best.ap[:NT, h, 0:8], in_=cand.ap[:NT, h, :]),
                      reads=[cand], writes=[best])
                kb.op("dve", lambda e, h=h: e.max_index(out=sel.ap[:NT, h, 0:8], in_max=best.ap[:NT, h, 0:8],
                                                        in_values=cand.ap[:NT, h, :]),
                      reads=[cand, best], writes=[sel])
                kb.op("dve", lambda e, h=h: e.match_replace(out=cwork.ap[:NT], in_to_replace=best.ap[:NT, h, 0:8],
                                                            in_values=cand.ap[:NT, h, :], imm_value=NEG),
                      reads=[cand, best], writes=[cwork])
                kb.op("dve", lambda e, h=h: e.max(out=best.ap[:NT, h, 8:16], in_=cwork.ap[:NT]),
                      reads=[cwork], writes=[best])
                kb.op("dve", lambda e, h=h: e.max_index(out=sel.ap[:NT, h, 8:16], in_max=best.ap[:NT, h, 8:16],
                                                        in_values=cwork.ap[:NT]),
                      reads=[cwork, best], writes=[sel])
                yield
            self_flat = sel[:NT].r("p h k -> p (h k)")
            kb.op("dve", lambda e: e.tensor_single_scalar(out=k1u.ap[:NT], in_=self_flat.ap, scalar=4,
                                                          op=ALU.logical_shift_right), reads=[sel], writes=[k1u])
            kb.op("dve", lambda e: e.tensor_single_scalar(out=k2u.ap[:NT], in_=self_flat.ap, scalar=15,
                                                          op=ALU.bitwise_and), reads=[sel], writes=[k2u])
            kb.copy("dve", k1f[:NT], k1u[:NT])
            kb.copy("dve", k2f[:NT], k2u[:NT])
            iota = C["iota16"]
            for (kf, half, dst) in ((k1f, 0, i1), (k2f, 1, i2)):
                kb.tt("dve", oh[:NT], iota[:NT].un(1).bc([NT, 128, 16]), kf[:NT].un(2).bc([NT, 128, 16]), ALU.is_equal)
                for h in range(8):
                    kb.tt("dve", oh[:NT, h * 16:(h + 1) * 16, :], oh[:NT, h * 16:(h + 1) * 16, :],
                          i4[:, h, half, :].un(1).bc([NT, 16, 16]), ALU.mult)
                kb.red("dve", dst[:NT], oh[:NT])
            kb.stt(eidf[:NT], i1[:NT], 128.0, i2[:NT], ALU.mult, ALU.add)
            if l > 0:
                kb.ts("dve", eidf[:NT], eidf[:NT], float(l * 16384), ALU.add)
            kb.ts("dve", eidf[:NT], eidf[:NT], 0.0, ALU.max, float(DEPTH * 16384 - 1), ALU.min)
            kb.copy("dve", eidx_all[:NT, ti, :], eidf[:NT])
            kb.tt("dve", gate[:NT], best[:NT], best[:NT, :, 0:1].bc([NT, 8, 16]), ALU.subtract)
            kb.act(gate[:NT], gate[:NT], AF.Exp)
            kb.red("dve", gsum[:NT], gate[:NT])
            kb.recip(gsum[:NT], gsum[:NT])
            kb.tt("dve", gate_all[:NT, ti, :].r("p (h k) -> p h k", h=8), gate[:NT],
                  gsum[:NT].un(2).bc([NT, 8, 16]), ALU.mult)

        tiles = [(i, i * 128, 128) for i in range(SEQ // 128)] + [(SEQ // 128, SEQ, NS)]
        if "b_tiles" in DEBUG:
            tiles = [t_ for t_ in tiles if t_[0] in DEBUG["b_tiles"]]
        run_lanes([b1_front(*tiles[0])])
        for ix in range(len(tiles)):
            lanes = [b1_topk(*tiles[ix])]
            if ix + 1 < len(tiles):
                lanes.append(b1_front(*tiles[ix + 1]))
            run_lanes(lanes)
        kb.barrier()

        cb.off = mark
        xBs = [cb.tile(128, [1024]) for _ in range(2)]
        h2s = [cb.tile(128, [1024]) for _ in range(2)]
        accs = [cb.tile(128, [1024]) for _ in range(2)]
        junk = cb.tile(128, [1024])
        ss2 = [cb.tile(128, [4]) for _ in range(2)]
        a_ts = [cb.tile(128, [128]) for _ in range(2)]
        g1 = cb.tile(128, [128])
        g2 = cb.tile(128, [128])
        w_ts = [cb.tile(128, [128]) for _ in range(2)]
        NRING = WCOLS // 1024
        ring = [T(wbuf[:, i * 1024:(i + 1) * 1024]) for i in range(NRING)]
        rp = [0]

        def nextbuf():
            b = ring[rp[0] % NRING]
            rp[0] += 1
            return b

        def tile2(ti, row0, NT):
            par = ti % 2
            xB, h2, accV, ss, a_t, w_t = xBs[par], h2s[par], accs[par], ss2[par], a_ts[par], w_ts[par]
            kb.dma("sp", xB[:NT], xmid[row0:row0 + NT, :])
            kb.memset("pool", ss[:NT], 0.0)
            kb.act(h2[:NT], xB[:NT], AF.Square, accum=ss[:NT, 0:1])
            kb.act(ss[:NT, 1:2], ss[:NT, 0:1], AF.Sqrt, bias=eps_t[:NT], scale=1.0 / D_MODEL)
            kb.recip(ss[:NT, 2:3], ss[:NT, 1:2])
            kb.stt(h2[:NT], xB[:NT], ss[:NT, 2:3], n2bc[:NT], ALU.mult, ALU.mult)
            kb.memset("pool", a_t[:NT], 0.0)
            for s in range(128):
                ub = nextbuf()
                kb.dma("pool", ub[:NT], D["peer_u"], indirect=eidx_all[:NT, ti, s:s + 1])
                kb.stt(junk[:NT], ub[:NT], 1.0, h2[:NT], ALU.mult, ALU.mult, accum=a_t[:NT, s:s + 1])
            kb.tt("dve", g1[:NT], a_t[:NT], a_t[:NT], ALU.mult)
            kb.ts("dve", g1[:NT], g1[:NT], 0.044715, ALU.mult, 1.0, ALU.add)
            kb.tt("dve", g1[:NT], g1[:NT], a_t[:NT], ALU.mult)
            kb.act(g2[:NT], g1[:NT], AF.Tanh, scale=0.7978845608028654)
            kb.ts("dve", g2[:NT], g2[:NT], 1.0, ALU.add, 0.5, ALU.mult)
            kb.tt("dve", g2[:NT], g2[:NT], a_t[:NT], ALU.mult)
            kb.tt("dve", w_t[:NT], g2[:NT], gate_all[:NT, ti, :], ALU.mult)
            for s in range(128):
                vb = nextbuf()
                kb.dma("pool", vb[:NT], D["peer_v"], indirect=eidx_all[:NT, ti, s:s + 1])
                if s == 0:
                    kb.ts("dve", accV[:NT], vb[:NT], w_t[:NT, 0:1], ALU.mult)
                else:
                    kb.stt(accV[:NT], vb[:NT], w_t[:NT, s:s + 1], accV[:NT], ALU.mult, ALU.add)
            kb.tt("dve", accV[:NT], xB[:NT], accV[:NT], ALU.add)
            if not is_last:
                kb.dma("sp", xdst[row0:row0 + NT, :], accV[:NT])
            else:
                kb.memset("pool", ss[:NT], 0.0)
                kb.act(junk[:NT], accV[:NT], AF.Square, accum=ss[:NT, 0:1])
                kb.act(ss[:NT, 1:2], ss[:NT, 0:1], AF.Sqrt, bias=eps_t[:NT], scale=1.0 / D_MODEL)
                kb.recip(ss[:NT, 2:3], ss[:NT, 1:2])
                kb.stt(accV[:NT], accV[:NT], ss[:NT, 2:3], fbc[:NT], ALU.mult, ALU.mult)
                if row0 < SEQ:
                    kb.dma("sp", D["y_p"][row0:row0 + NT, :], accV[:NT], is_output=True)
                else:
                    kb.dma("sp", D["y_s"][:, :], accV[:NT], is_output=True)

        for (ti, row0, NT) in tiles:
            tile2(ti, row0, NT)

    for l in range(DEPTH):
        if DEBUG.get("only_layer0") and l > 0:
            break
        if l == 0:
            phaseA(l, D["xp"], D["xs"])
        else:
            phaseA(l, xnext[0:SEQ, :], xnext[SEQ:NTOK, :])
        kb.barrier()
        if DEBUG.get("skip_b"):
            continue
        phaseB(l, xnext, l == DEPTH - 1)
        kb.barrier()
    kb.finish()
    return nc, kb


_CACHE = {}


def kernel(x_prompt, x_sample, state_delta, state_delta_conv, state_pool, state_ret, state_conv,
           norm1, w_in, delta_conv_w, delta_a_log, delta_dt_bias, delta_norm_w, pool_w, pool_scale,
           conv_dw_w, conv_dw_b, conv_ln_w, conv_ln_b, conv_pw_w, w_out, norm2,
           peer_wq, peer_keys, peer_u, peer_v, final_norm):
    f = lambda a: np.ascontiguousarray(np.asarray(a, dtype=np.float32))
    if "nc" not in _CACHE:
        _CACHE["nc"] = build_program()
    nc, kb = _CACHE["nc"]
    consts = host_constants()
    keysT = f(np.transpose(np.asarray(peer_keys), (0, 1, 2, 4, 3)).reshape(DEPTH, 16, 128, 128))
    dconv_wT = f(np.transpose(np.asarray(delta_conv_w), (0, 2, 1)))
    dw_wT = f(np.transpose(np.asarray(conv_dw_w), (0, 2, 1)))
    cv = np.stack([np.asarray(conv_dw_b), np.asarray(conv_ln_w), np.asarray(conv_ln_b), np.asarray(pool_scale)], 1)
    colvecs = f(np.transpose(cv.reshape(DEPTH, 4, 2, 128), (0, 3, 1, 2)).reshape(DEPTH, 128, 8))
    shared = dict(
        norm1=f(norm1), norm2=f(norm2), final_norm=f(np.asarray(final_norm).reshape(1, D_MODEL)),
        w_in=f(w_in), w_out=f(w_out), peer_wq=f(peer_wq), keysT=keysT, peer_u=f(peer_u).reshape(DEPTH * 16384, D_MODEL), peer_v=f(peer_v).reshape(DEPTH * 16384, D_MODEL),
        dconv_wT=dconv_wT, dw_wT=dw_wT, colvecs=colvecs, pw_w=f(conv_pw_w), pool_w=f(pool_w),
        a_log=f(delta_a_log), dt_bias=f(delta_dt_bias), dnorm_w=f(delta_norm_w),
    )
    for k, v in consts.items():
        shared["c_" + k] = f(v)
    xp = np.asarray(x_prompt)
    xs = np.asarray(x_sample)
    in_maps = []
    for c in range(NCORES):
        sl = slice(c * NS, (c + 1) * NS)
        m = dict(shared)
        m["xp"] = f(xp[c])
        m["xs"] = f(xs[sl, 0, :])
        m["st_delta"] = f(np.asarray(state_delta)[:, sl])
        m["st_dconv"] = f(np.asarray(state_delta_conv)[:, sl])
        m["st_pool"] = f(np.asarray(state_pool)[:, sl])
        m["st_ret"] = f(np.asarray(state_ret)[:, sl])
        m["st_conv"] = f(np.asarray(state_conv)[:, sl])
        in_maps.append(m)
    res = run_bass_kernel_spmd(nc, in_maps, core_ids=list(range(NCORES)))
    R = res.results
    y_p = np.stack([R[c]["y_p"] for c in range(NCORES)], 0)
    y_s = np.concatenate([R[c]["y_s"] for c in range(NCORES)], 0)[:, None, :]
    outs = [y_p, y_s]
    for nm in ("d_p", "dc_p", "pl_p", "r_p", "c_p"):
        outs.append(np.stack([R[c][nm] for c in range(NCORES)], 1))
    for nm in ("d_s", "dc_s", "pl_s", "r_s", "c_s"):
        outs.append(np.concatenate([R[c][nm] for c in range(NCORES)], 1))
    return tuple(np.ascontiguousarray(o.astype(np.float32)) for o in outs)
```

```python
import numpy as np
import concourse.bass as bass
import concourse.mybir as mybir
from concourse.bass_utils import run_bass_kernel_spmd

F32 = mybir.dt.float32
I32 = mybir.dt.int32
U32 = mybir.dt.uint32
BF16 = mybir.dt.bfloat16
ALU = mybir.AluOpType
AF = mybir.ActivationFunctionType
AX = mybir.AxisListType

D_MODEL = 1024
SEQ = 2048
DEPTH = 2
NS = 16
NCORES = 8
IN_W = 2824
EPS = 1e-6
NEG = -1.0e30
DEBUG = {}
NO_DRAIN = True
DELTA_PRIO = 2


class Buf:
    __slots__ = ("w", "r", "bank")

    def __init__(self):
        self.w = None
        self.r = {}
        self.bank = None


class T:
    __slots__ = ("ap", "buf")

    def __init__(self, ap, buf=None):
        self.ap = ap
        self.buf = buf if buf is not None else Buf()

    def __getitem__(self, key):
        return T(self.ap[key], self.buf)

    def bc(self, shape):
        return T(self.ap.to_broadcast(list(shape)), self.buf)

    def un(self, axis):
        return T(self.ap.unsqueeze(axis), self.buf)

    def r(self, pat, **kw):
        return T(self.ap.rearrange(pat, **kw), self.buf)

    def bitcast(self, dt):
        return T(self.ap.bitcast(dt), self.buf)


def _ap(x):
    return x.ap if isinstance(x, T) else x


class KB:
    NDMA = 24

    def __init__(self, nc):
        self.nc = nc
        self.eng = dict(pe=nc.tensor, dve=nc.vector, act=nc.scalar, pool=nc.gpsimd, sp=nc.sync)
        self.sem = {("e", e): nc.alloc_semaphore("sem_" + e) for e in self.eng}
        self.cnt = {e: 0 for e in self.eng}
        self.seen = {e: {} for e in self.eng}
        self.dq = {}
        for q in ("sp", "pool"):
            sl = []
            for i in range(self.NDMA):
                key = ("d", q, i)
                self.sem[key] = nc.alloc_semaphore("dma_%s_%d" % (q, i))
                sl.append(key)
            self.dq[q] = dict(slots=sl, uses=[0] * self.NDMA, i=0)
        for b in range(8):
            self.sem[("p", b)] = nc.alloc_semaphore("sem_pe_bank%d" % b)
        self.pe_cnt = [0] * 8
        self.out_events = []
        self.ninst = 0

    def _waits(self, eng, reads, writes, self_sync):
        evs = {}
        for b in reads:
            if b.w is not None:
                k, v = b.w
                if evs.get(k, 0) < v:
                    evs[k] = v
            if b.bank is not None:
                for k, v in b.r.items():
                    if k != ("e", eng) and evs.get(k, 0) < v:
                        evs[k] = v
        for b in writes:
            if b.w is not None:
                k, v = b.w
                if evs.get(k, 0) < v:
                    evs[k] = v
            for k, v in b.r.items():
                if evs.get(k, 0) < v:
                    evs[k] = v
        e = self.eng[eng]
        seen = self.seen[eng]
        for k, v in evs.items():
            if k == ("e", eng) and not self_sync:
                continue
            if eng == "pe" and k[0] == "p" and not self_sync:
                continue
            if seen.get(k, 0) < v:
                e.wait_ge(self.sem[k], v)
                seen[k] = v
                self.ninst += 1

    def _mark(self, ev, reads, writes):
        k, v = ev
        for b in reads:
            if b.r.get(k, 0) < v:
                b.r[k] = v
        for b in writes:
            b.w = ev
            b.r = {}

    def op(self, eng, fn, reads=(), writes=(), self_sync=True):
        reads = [t.buf for t in reads if isinstance(t, T)]
        writes = [t.buf for t in writes if isinstance(t, T)]
        self._waits(eng, reads, writes, self_sync)
        inst = fn(self.eng[eng])
        self.cnt[eng] += 1
        self.ninst += 1
        if eng == "pe":
            bank = writes[0].bank
            self.pe_cnt[bank] += 1
            key = ("p", bank)
            inst.then_inc(self.sem[key], 1)
            self._mark((key, self.pe_cnt[bank]), reads, writes)
        else:
            inst.then_inc(self.sem[("e", eng)], 1)
            self._mark((("e", eng), self.cnt[eng]), reads, writes)

    def dma(self, q, out, in_, is_output=False, indirect=None, **kw):
        reads = [t.buf for t in (in_, indirect) if isinstance(t, T)]
        writes = [t.buf for t in (out,) if isinstance(t, T)]
        dq = self.dq[q]
        si = dq["i"] % self.NDMA
        dq["i"] += 1
        key = dq["slots"][si]
        prev = dq["uses"][si] * 16
        seen = self.seen[q]
        self._waits(q, reads, writes, True)
        if prev and seen.get(key, 0) < prev:
            self.eng[q].wait_ge(self.sem[key], prev)
            seen[key] = prev
        if indirect is not None:
            inst = self.nc.gpsimd.indirect_dma_start(
                out=_ap(out), out_offset=None, in_=_ap(in_),
                in_offset=bass.IndirectOffsetOnAxis(ap=_ap(indirect), axis=0), **kw)
        else:
            inst = self.eng[q].dma_start(out=_ap(out), in_=_ap(in_), **kw)
        inst.then_inc(self.sem[key], 16)
        dq["uses"][si] += 1
        ev = (key, dq["uses"][si] * 16)
        self.ninst += 1
        self._mark(ev, reads, writes)
        if is_output:
            self.out_events.append(ev)

    def _all_events(self):
        evs = {}
        for e, c in self.cnt.items():
            if c and e != "pe":
                evs[("e", e)] = c
        for b, c in enumerate(self.pe_cnt):
            if c:
                evs[("p", b)] = c
        for q, dq in self.dq.items():
            for key, u in zip(dq["slots"], dq["uses"]):
                if u:
                    evs[key] = u * 16
        return evs

    def barrier(self):
        evs = self._all_events()
        sp = self.eng["sp"]
        seen = self.seen["sp"]
        for k, v in evs.items():
            if k == ("e", "sp"):
                continue
            if seen.get(k, 0) < v:
                sp.wait_ge(self.sem[k], v)
                seen[k] = v
        inst = sp.nop()
        self.cnt["sp"] += 1
        inst.then_inc(self.sem[("e", "sp")], 1)
        v = self.cnt["sp"]
        for e in self.eng:
            if e == "sp":
                continue
            self.eng[e].wait_ge(self.sem[("e", "sp")], v)
            self.seen[e][("e", "sp")] = v
            for k, vv in evs.items():
                if self.seen[e].get(k, 0) < vv:
                    self.seen[e][k] = vv

    def finish(self):
        evs = self._all_events()
        sp = self.eng["sp"]
        for k, v in evs.items():
            if k == ("e", "sp"):
                continue
            if self.seen["sp"].get(k, 0) < v:
                sp.wait_ge(self.sem[k], v)
                self.seen["sp"][k] = v

    def tt(self, eng, out, in0, in1, op):
        self.op(eng, lambda e: e.tensor_tensor(out=out.ap, in0=in0.ap, in1=in1.ap, op=op),
                reads=[in0, in1], writes=[out])

    def ts(self, eng, out, in0, s1, op0, s2=None, op1=None):
        if op1 is None:
            self.op(eng, lambda e: e.tensor_scalar(out=out.ap, in0=in0.ap, scalar1=_ap(s1), scalar2=None, op0=op0),
                    reads=[in0, s1], writes=[out])
        else:
            self.op(eng, lambda e: e.tensor_scalar(out=out.ap, in0=in0.ap, scalar1=_ap(s1), scalar2=_ap(s2),
                                                   op0=op0, op1=op1),
                    reads=[in0, s1, s2], writes=[out])

    def stt(self, out, in0, scalar, in1, op0, op1, accum=None):
        if accum is None:
            self.op("dve", lambda e: e.scalar_tensor_tensor(out=out.ap, in0=in0.ap, scalar=_ap(scalar), in1=in1.ap,
                                                            op0=op0, op1=op1),
                    reads=[in0, scalar, in1], writes=[out])
        else:
            self.op("dve", lambda e: e.scalar_tensor_tensor(out=out.ap, in0=in0.ap, scalar=_ap(scalar), in1=in1.ap,
                                                            op0=op0, op1=op1, accum_out=accum.ap),
                    reads=[in0, scalar, in1, accum], writes=[out, accum])

    def act(self, out, in_, func, bias=None, scale=1.0, accum=None):
        kw = {}
        if bias is not None:
            kw["bias"] = _ap(bias)
        if accum is not None:
            kw["accum_out"] = accum.ap
        self.op("act", lambda e: e.activation(out=out.ap, in_=in_.ap, func=func, scale=_ap(scale), **kw),
                reads=[in_, bias, scale, accum], writes=[out, accum] if accum is not None else [out])

    def copy(self, eng, out, in_):
        if eng == "act":
            self.op("act", lambda e: e.copy(out=out.ap, in_=in_.ap), reads=[in_], writes=[out])
        else:
            self.op(eng, lambda e: e.tensor_copy(out=out.ap, in_=in_.ap), reads=[in_], writes=[out])

    def memset(self, eng, out, val):
        self.op(eng, lambda e: e.memset(out.ap, val), writes=[out])

    def red(self, eng, out, in_, op=ALU.add):
        self.op(eng, lambda e: e.tensor_reduce(out=out.ap, in_=in_.ap, axis=AX.X, op=op), reads=[in_], writes=[out])

    def recip(self, out, in_):
        self.op("dve", lambda e: e.reciprocal(out=out.ap, in_=in_.ap), reads=[in_], writes=[out])

    def _pe_rowgroup(self, out, lhsT, is_tr=False):
        def rnd(n):
            return 32 if n <= 32 else (64 if n <= 64 else 128)
        k = rnd(lhsT.ap.partition_size())
        m = rnd(out.ap.partition_size())
        sig = (k, lhsT.ap.base_partition() if k < 128 else 0, m, out.ap.base_partition() if m < 128 else 0,
               is_tr, str(lhsT.ap.dtype))
        last = getattr(self, "_last_pe_sig", None)
        if last is not None and (getattr(self, "pe_serial", False) or (last != sig and not DEBUG.get("no_drain", NO_DRAIN))):
            for b in range(8):
                key, v = ("p", b), self.pe_cnt[b]
                if v and self.seen["pe"].get(key, 0) < v:
                    self.eng["pe"].wait_ge(self.sem[key], v)
                    self.seen["pe"][key] = v
                    self.ninst += 1
        self._last_pe_sig = sig

    def mm(self, out, lhsT, rhs, start=True, stop=True):
        self._pe_rowgroup(out, lhsT)
        self.op("pe", lambda e: e.matmul(out.ap, lhsT=lhsT.ap, rhs=rhs.ap, start=start, stop=stop),
                reads=[lhsT, rhs], writes=[out], self_sync=False)

    def tr(self, out, in_, ident):
        self._pe_rowgroup(out, in_, True)
        self.op("pe", lambda e: e.transpose(out=out.ap, in_=in_.ap, identity=ident.ap),
                reads=[in_, ident], writes=[out], self_sync=False)


class Carver:
    def __init__(self, base_ap, ncols):
        self.base = base_ap
        self.ncols = ncols
        self.off = 0

    def tile(self, parts, shape, dt=F32):
        n = int(np.prod(shape))
        assert self.off + n <= self.ncols, ("work region overflow", self.off, n, self.ncols)
        ap = self.base[0:parts, self.off:self.off + n]
        self.off += n
        if dt != F32:
            ap = ap.bitcast(dt)
        if len(shape) == 2:
            ap = ap.rearrange("p (a b) -> p a b", a=shape[0])
        elif len(shape) == 3:
            ap = ap.rearrange("p (a b c) -> p a b c", a=shape[0], b=shape[1])
        return T(ap)


def host_constants():
    c = {}
    c["ident"] = np.eye(128, dtype=np.float32)
    c["ones"] = np.ones((128, 128), np.float32)
    c["negones"] = -np.ones((128, 128), np.float32)
    bo = np.zeros((128, 128), np.float32)
    bo[:64, :64] = 1
    bo[64:, 64:] = 1
    c["blockones"] = bo
    i = np.arange(64)
    c["ltriT"] = (i[:, None] <= i[None, :]).astype(np.float32)
    c["negmask"] = np.where(i[None, :] <= i[:, None], 0.0, NEG).astype(np.float32)
    c["negstrict"] = np.where(i[None, :] < i[:, None], -1.0, 0.0).astype(np.float32)
    pm = np.zeros((128, 128), np.float32)
    for m in range(128):
        blk, r = divmod(m, 64)
        src = blk * 64 + (r + 32) % 64
        pm[src, m] = 1
    c["perm"] = pm
    half = 32
    inv = (1.0 / (10000.0 ** (np.arange(half, dtype=np.float32) / np.float32(half)))).astype(np.float32)
    pos = np.concatenate([np.arange(SEQ, dtype=np.float32), np.array([16384.0], np.float32)])
    ang = (pos[:, None] * inv[None, :]).astype(np.float32)
    cos = np.cos(ang).astype(np.float32).T
    sin = np.sin(ang).astype(np.float32).T
    C = np.concatenate([cos, cos, cos, cos], 0)
    S = np.concatenate([-sin, sin, -sin, sin], 0)
    c["ropeC"] = np.ascontiguousarray(C)
    c["ropeS"] = np.ascontiguousarray(S)
    lg = np.log1p(-np.exp2(-5.0 - np.arange(4, dtype=np.float32))).astype(np.float32)
    dmT = np.zeros((2, 64, 4, 64), np.float32)
    eGrow = np.zeros((2, 128, 2, 64), np.float32)
    eGr = np.zeros((2, 64, 4), np.float32)
    dl = np.zeros((2, 128, 4), np.float32)
    for h in range(4):
        d = (i[None, :] - i[:, None]).astype(np.float32)
        dmT[0, :, h, :] = np.where(d >= 0, np.exp(np.minimum(d, 64) * lg[h]), 0.0)
        dmT[1, :, h, :] = 0.0
        dmT[1, 0, h, 0] = 1.0
        eGr[0, :, h] = np.exp((63 - i).astype(np.float32) * lg[h])
        eGr[1, :, h] = 1.0
        dl[0, :, h] = np.exp(np.float32(64.0) * lg[h])
        dl[1, :, h] = np.exp(lg[h])
        k, pb = divmod(h, 2)
        eGrow[0, pb * 64:(pb + 1) * 64, k, :] = np.exp((i + 1).astype(np.float32) * lg[h])[None, :]
        eGrow[1, pb * 64:(pb + 1) * 64, k, :] = np.exp(lg[h])
    c["ret_dmT"] = dmT
    c["ret_eGrow"] = eGrow
    c["ret_eGr"] = eGr
    c["ret_dl"] = dl
    wins = [2, 4, 8, 16]
    pinv = np.zeros((2, 128, 2, 64), np.float32)
    t = np.arange(64)
    for k in range(2):
        for pbk in range(2):
            w = wins[2 * k + pbk]
            pinv[0, pbk * 64:(pbk + 1) * 64, k, :] = (1.0 / np.minimum(w, t + 1).astype(np.float32))[None, :]
            pinv[1, pbk * 64:(pbk + 1) * 64, k, :] = np.float32(1.0 / w)
    c["pool_inv"] = pinv
    c["iota16"] = np.broadcast_to(np.arange(16, dtype=np.float32)[None, :], (128, 16)).copy()
    return c


CONST_SHAPES = None


def build_program():
    nc = bass.Bass("TRN2", target_bir_lowering=False)
    kb = KB(nc)
    D = {}

    def din(name, shape, dt=F32):
        D[name] = nc.dram_tensor(name, list(shape), dt, kind="ExternalInput").ap()

    def dout(name, shape):
        D[name] = nc.dram_tensor(name, list(shape), F32, kind="ExternalOutput").ap()

    din("xp", [SEQ, D_MODEL])
    din("xs", [NS, D_MODEL])
    din("st_delta", [DEPTH, NS, 4, 64, 64])
    din("st_dconv", [DEPTH, NS, 3, 768])
    din("st_pool", [DEPTH, NS, 15, 256])
    din("st_ret", [DEPTH, NS, 4, 64, 64])
    din("st_conv", [DEPTH, NS, 30, 256])
    din("norm1", [DEPTH, D_MODEL])
    din("norm2", [DEPTH, D_MODEL])
    din("final_norm", [1, D_MODEL])
    din("w_in", [DEPTH, D_MODEL, IN_W])
    din("w_out", [DEPTH, D_MODEL, D_MODEL])
    din("peer_wq", [DEPTH, D_MODEL, 2048])
    din("keysT", [DEPTH, 16, 128, 128])
    din("peer_u", [DEPTH * 16384, D_MODEL])
    din("peer_v", [DEPTH * 16384, D_MODEL])
    din("dconv_wT", [DEPTH, 768, 4])
    din("dw_wT", [DEPTH, 256, 31])
    din("colvecs", [DEPTH, 128, 8])
    din("pw_w", [DEPTH, 256, 256])
    din("pool_w", [DEPTH, 4, 64, 64])
    din("a_log", [DEPTH, 4])
    din("dt_bias", [DEPTH, 4])
    din("dnorm_w", [DEPTH, 64])
    consts = host_constants()
    for k, v in consts.items():
        din("c_" + k, v.shape)

    dout("y_p", [SEQ, D_MODEL])
    dout("y_s", [NS, D_MODEL])
    dout("d_p", [DEPTH, 4, 64, 64])
    dout("dc_p", [DEPTH, 3, 768])
    dout("pl_p", [DEPTH, 15, 256])
    dout("r_p", [DEPTH, 4, 64, 64])
    dout("c_p", [DEPTH, 30, 256])
    dout("d_s", [DEPTH, NS, 4, 64, 64])
    dout("dc_s", [DEPTH, NS, 3, 768])
    dout("pl_s", [DEPTH, NS, 15, 256])
    dout("r_s", [DEPTH, NS, 4, 64, 64])
    dout("c_s", [DEPTH, NS, 30, 256])
    NTOK = SEQ + NS
    xmid = nc.dram_tensor("xmid", [NTOK, D_MODEL], F32, kind="Internal").ap()
    xnext = nc.dram_tensor("xnext", [NTOK, D_MODEL], F32, kind="Internal").ap()

    WCOLS = 30784
    wbuf = nc.alloc_sbuf_tensor("wbuf", [128, WCOLS], F32)
    bW = Buf()
    W_in = T(wbuf[:, 0:8 * IN_W].rearrange("p (k e) -> p k e", k=8), bW)
    W_out = T(wbuf[:, 8 * IN_W:8 * IN_W + 8192].rearrange("p (k e) -> p k e", k=8), bW)
    wq = T(wbuf[:, 0:16384].rearrange("p (k e) -> p k e", k=8), bW)
    keysT = T(wbuf[:, 16384:18432].rearrange("p (g n) -> p g n", g=16), bW)
    NTB = SEQ // 128 + 1

    def sb(name, shape, dt=F32):
        return T(nc.alloc_sbuf_tensor("sb_" + name, list(shape), dt)[:])

    C = {}
    for k, v in consts.items():
        if v.ndim == 2:
            C[k] = sb("sc_" + k, v.shape)
        else:
            shp = list(v.shape)
            C[k] = sb("sc_" + k, [shp[1], shp[0]] + shp[2:])
    eps_t = sb("eps_t", [128, 1])
    dconv_w = sb("dconv_w", [128, 6, 4])
    dw_w = sb("dw_w", [128, 2, 31])
    colv = sb("colv", [128, 8])
    pw_w = sb("pw_w", [128, 2, 256])
    pool_bd = sb("pool_bd", [128, 2, 128])
    alog = sb("alog", [64, 4])
    dtb = sb("dtb", [64, 4])
    negA = sb("negA", [64, 4])
    dnorm = sb("dnorm", [64, 64])
    WORKC = 14440
    work = nc.alloc_sbuf_tensor("work", [128, WORKC], F32)

    psum_raw = [nc.alloc_psum_tensor("ps%d" % i, [128, 512], F32) for i in range(8)]
    BK = [T(psum_raw[i][:]) for i in range(8)]
    for i_ in range(8):
        BK[i_].buf.bank = i_

    def BKV(bank, col0, ncols):
        assert col0 + ncols <= 512
        return BK[bank][:, col0:col0 + ncols]

    def PR(t):
        pass

    for k, v in consts.items():
        if k in ("ropeC", "ropeS"):
            continue
        if v.ndim == 2:
            kb.dma("sp", C[k], D["c_" + k])
        else:
            for kind in range(v.shape[0]):
                kb.dma("sp", C[k][:, kind], D["c_" + k][kind])
    kb.memset("pool", eps_t, EPS)
    ident = C["ident"]
    ones = C["ones"]

    def interleave(gens):
        gens = list(gens)
        if DEBUG.get("seq_heads"):
            for g in gens:
                for _ in g:
                    pass
            yield
            return
        while gens:
            nxt = []
            for g in gens:
                try:
                    next(g)
                    nxt.append(g)
                except StopIteration:
                    pass
            gens = nxt
            yield

    def run_lanes(gens):
        if DEBUG.get("seq_lanes"):
            for g in gens:
                for _ in g:
                    pass
            return
        for _ in interleave(gens):
            pass

    def phaseA(l, xin_p, xin_s):
        ca = Carver(work, WORKC)
        NWI = 8 * IN_W // 2
        W_in = T(wbuf[:, 0:NWI].bitcast(BF16).rearrange("p (k e) -> p k e", k=8), bW)
        W_out = T(wbuf[:, NWI:NWI + 4096].bitcast(BF16).rearrange("p (k e) -> p k e", k=8), bW)
        wa = Carver(wbuf, WCOLS)
        wa.off = NWI + 4096
        stg = [wa.tile(128, [IN_W]) for _ in range(2)]
        cast_eng = ["act", "pool", "dve"]
        for kc in range(8):
            st = stg[kc % 2]
            kb.dma("sp", st, D["w_in"][l, kc * 128:(kc + 1) * 128, :])
            kb.copy(cast_eng[kc % 3], W_in[:, kc, :], st)
        for kc in range(8):
            st = stg[kc % 2]
            kb.dma("sp", st[:, 0:1024], D["w_out"][l, kc * 128:(kc + 1) * 128, :])
            kb.copy(cast_eng[kc % 3], W_out[:, kc, :], st[:, 0:1024])
        n1bc = ca.tile(64, [1024])
        kb.dma("sp", n1bc, D["norm1"][l:l + 1, :].to_broadcast([64, 1024]))
        kb.dma("sp", dconv_w, D["dconv_wT"][l].rearrange("(k p) j -> p k j", p=128))
        kb.dma("sp", dw_w, D["dw_wT"][l].rearrange("(k p) j -> p k j", p=128))
        kb.dma("sp", colv, D["colvecs"][l])
        kb.dma("sp", pw_w, D["pw_w"][l].rearrange("(k p) d -> p k d", p=128))
        kb.memset("pool", pool_bd, 0.0)
        for g in range(4):
            k, pbk = divmod(g, 2)
            kb.dma("sp", pool_bd[pbk * 64:(pbk + 1) * 64, k, pbk * 64:(pbk + 1) * 64], D["pool_w"][l, g])
        kb.dma("sp", alog, D["a_log"][l:l + 1, :].to_broadcast([64, 4]))
        kb.dma("sp", dtb, D["dt_bias"][l:l + 1, :].to_broadcast([64, 4]))
        kb.dma("sp", dnorm, D["dnorm_w"][l:l + 1, :].to_broadcast([64, 64]))
        kb.act(negA, alog, AF.Exp)
        kb.ts("dve", negA, negA, -1.0, ALU.mult)
        dw_b, ln_w, ln_b, pscale = colv[:, 0:2], colv[:, 2:4], colv[:, 4:6], colv[:, 6:8]

        xA = ca.tile(64, [1024])
        hA = ca.tile(64, [1024])
        xo = ca.tile(64, [1024])
        ss = ca.tile(64, [4])
        hT = T(ca.tile(128, [8 * 32]).ap.bitcast(BF16).rearrange("p (k t) -> p k t", k=8))
        full_d = ca.tile(128, [6, 67])
        full_p = ca.tile(128, [2, 79])
        full_c = ca.tile(128, [2, 94])
        zT2 = ca.tile(128, [8, 64])
        ztok = ca.tile(64, [776])
        qkvc = ca.tile(128, [6, 64])
        qkvs = ca.tile(128, [6, 64])
        sq4 = ca.tile(128, [4, 64])
        rinv = ca.tile(128, [4, 64])
        qkn = ca.tile(128, [4, 64])
        kv_tok = ca.tile(64, [512])
        ab = ca.tile(64, [16, 4])
        GL = ca.tile(128, [4])
        dl = ca.tile(128, [4])
        Sd = ca.tile(128, [2, 64])
        Sr = ca.tile(128, [2, 64])
        HB = []
        for h in range(4):
            d = {}
            for nm in ("diagG", "dmm", "dm", "Araw", "Nm0", "Nm1", "Nt0", "Nt1", "Aqk", "AqkT", "Tt", "t1",
                       "rhsv", "vnew", "QSs", "Kd", "rAqkT", "rQSs"):
                d[nm] = wa.tile(64, [64])
            HB.append(d)
        o_tok = ca.tile(64, [256])
        osq = ca.tile(64, [256])
        ost = ca.tile(64, [8])
        ogate = ca.tile(64, [256])
        rsq = ca.tile(64, [256])
        rst = ca.tile(64, [8])
        rgate = ca.tile(64, [256])
        mixT = T(ca.tile(128, [8 * 32]).ap.bitcast(BF16).rearrange("p (k t) -> p k t", k=8))
        ps2 = ca.tile(128, [2, 79])
        ps4 = ca.tile(128, [2, 79])
        ps8 = ca.tile(128, [79])
        ps16 = ca.tile(128, [79])
        pw = ca.tile(128, [2, 64])
        ropeC = ca.tile(128, [64])
        ropeS = ca.tile(128, [64])
        ropeCs = [ropeC, ca.tile(128, [64])]
        ropeSs = [ropeS, ca.tile(128, [64])]
        rot1 = ca.tile(128, [4, 64])
        rot = ca.tile(128, [4, 64])
        QdT = ca.tile(128, [2, 64])
        Kd_tok = ca.tile(64, [256])
        r_tok = ca.tile(64, [256])
        rcen = ca.tile(64, [256])
        sg = ca.tile(128, [2, 64])
        cacc = ca.tile(128, [2, 64])
        ccen = ca.tile(128, [2, 64])
        csq = ca.tile(128, [2, 64])
        crs = ca.tile(128, [64])
        chn = ca.tile(128, [2, 64])
        hstage = [wa.tile(32, [768]), wa.tile(32, [256]), wa.tile(32, [256])]
        xAs = [xA, wa.tile(64, [1024])]
        zT1s = [wa.tile(128, [8, 64]) for _ in range(2)]
        zT2s = [zT2, wa.tile(128, [8, 64])]
        ztoks = [ztok, wa.tile(64, [776])]
        hload = xo

        def hist_in(full, nk, H, src):
            kb.dma("sp", hload[:H, 0:nk * 128], src)
            ps = BKV(6, 0, nk * H)
            for k in range(nk):
                kb.tr(ps[:, k * H:(k + 1) * H], hload[:H, k * 128:(k + 1) * 128], ident[:H, :H])
            kb.copy("act", full[:, :, 0:H], ps[:, 0:nk * H].r("p (k h) -> p k h", k=nk))
            PR(ps)

        def hist_out(full, nk, H, NT, dst, stage):
            for k in range(nk):
                ps = BKV(6, 384, 128)
                kb.tr(ps[:H, 0:128], full[:, k, NT:NT + H], ident)
                kb.copy("act", stage[:H, k * 128:(k + 1) * 128], ps[:H, 0:128])
            kb.dma("sp", dst, stage[:H, 0:nk * 128], is_output=True)

        def front_gen(kind, idx, par):
            NT = 64 if kind == "p" else 1
            row0 = idx * 64 if kind == "p" else SEQ + idx
            pcol = idx * 64 if kind == "p" else SEQ
            rkw = dict(allow_slow_non_contiguous=True) if NT == 1 else {}
            xA, zT1, zT2, ztok, ropeC, ropeS = xAs[par], zT1s[par], zT2s[par], ztoks[par], ropeCs[par], ropeSs[par]
            kb.dma("sp", xA[:NT], xin_p[row0:row0 + NT, :] if kind == "p" else xin_s[idx:idx + 1, :])
            kb.dma("sp", ropeC[:, :NT], D["c_ropeC"][:, pcol:pcol + NT], **rkw)
            kb.dma("sp", ropeS[:, :NT], D["c_ropeS"][:, pcol:pcol + NT], **rkw)
            kb.memset("pool", ss[:NT], 0.0)
            yield
            kb.act(hA[:NT], xA[:NT], AF.Square, accum=ss[:NT, 0:1])
            yield
            kb.act(ss[:NT, 1:2], ss[:NT, 0:1], AF.Sqrt, bias=eps_t[:NT], scale=1.0 / D_MODEL)
            yield
            kb.recip(ss[:NT, 2:3], ss[:NT, 1:2])
            yield
            kb.stt(hA[:NT], xA[:NT], ss[:NT, 2:3], n1bc[:NT], ALU.mult, ALU.mult)
            yield
            ps = BKV(7, 0, 512)
            for kc in range(8):
                kb.tr(ps[:, kc * NT:(kc + 1) * NT], hA[:NT, kc * 128:(kc + 1) * 128], ident[:NT, :NT])
            yield
            kb.copy("act", hT[:, :, :NT], ps[:, 0:8 * NT].r("p (k t) -> p k t", k=8))
            yield

            def zfm(ps, slot, col0):
                for kc in range(8):
                    kb.mm(ps[:, slot * NT:(slot + 1) * NT], W_in[:, kc, col0:col0 + 128], hT[:, kc, :NT],
                          start=(kc == 0), stop=(kc == 7))
            ps = BKV(7, 0, 512)
            for k in range(6):
                zfm(ps, k, k * 128)
                if k % 2 == 1:
                    yield
            for k in range(2):
                zfm(ps, 6 + k, 1032 + k * 128)
            yield
            kb.copy("act", zT1[:, :, :NT], ps[:, 0:8 * NT].r("p (k t) -> p k t", k=8))
            yield
            ps = BKV(7, 0, 512)
            for k in range(4):
                zfm(ps, k, 1288 + k * 128)
                if k % 2 == 1:
                    yield
            for k in range(4):
                zfm(ps, 4 + k, 2312 + k * 128)
                if k % 2 == 1:
                    yield
            kb.copy("act", zT2[:, :, :NT], ps[:, 0:8 * NT].r("p (k t) -> p k t", k=8))
            yield
            ps = BKV(7, 0, 512)
            for kc in range(8):
                kb.mm(ps[:NT, 0:264], hT[:, kc, :NT], W_in[:, kc, 768:1032], start=(kc == 0), stop=(kc == 7))
            yield
            kb.copy("act", ztok[:NT, 0:264], ps[:NT, 0:264])
            yield
            ps = BKV(7, 0, 512)
            for kc in range(8):
                kb.mm(ps[:NT, 0:512], hT[:, kc, :NT], W_in[:, kc, 1800:2312], start=(kc == 0), stop=(kc == 7))
            yield
            kb.copy("act", ztok[:NT, 264:776], ps[:NT, 0:512])

        def tile(kind, idx, par, nxt_front):
            NT = 64 if kind == "p" else 1
            kk = 0 if kind == "p" else 1
            first = kind == "p" and idx == 0
            last = (kind == "s") or idx == SEQ // 64 - 1
            row0 = idx * 64 if kind == "p" else SEQ + idx
            xA, zT1, zT2, ztok, ropeC, ropeS = xAs[par], zT1s[par], zT2s[par], ztoks[par], ropeCs[par], ropeSs[par]
            if first:
                kb.memset("pool", full_d[:, :, 0:3], 0.0)
                kb.memset("pool", full_p[:, :, 0:15], 0.0)
                kb.memset("pool", full_c[:, :, 0:30], 0.0)
                kb.memset("pool", Sd, 0.0)
                kb.memset("pool", Sr, 0.0)
            elif kind == "p":
                kb.copy("pool", full_d[:, :, 0:3], full_d[:, :, 64:67])
                kb.copy("pool", full_p[:, :, 0:15], full_p[:, :, 64:79])
                kb.copy("pool", full_c[:, :, 0:30], full_c[:, :, 64:94])
            else:
                hist_in(full_d, 6, 3, D["st_dconv"][l, idx])
                hist_in(full_p, 2, 15, D["st_pool"][l, idx])
                hist_in(full_c, 2, 30, D["st_conv"][l, idx])
                for pb_ in range(2):
                    pbs_ = slice(pb_ * 64, (pb_ + 1) * 64)
                    kb.dma("sp", Sd[pbs_], D["st_delta"][l, idx].rearrange("(q b) d e -> b d q e", b=2)[pb_])
                    kb.dma("sp", Sr[pbs_], D["st_ret"][l, idx].rearrange("(q b) d e -> b d q e", b=2)[pb_])
            kb.copy("act", full_d[:, :, 3:3 + NT], zT1[:, 0:6, :NT])
            kb.copy("pool", full_p[:, :, 15:15 + NT], zT1[:, 6:8, :NT])

            a_t, b_t = ztok[:NT, 256:260], ztok[:NT, 260:264]
            beta, nbeta, xsp, axs, esp, lsp, gg, Gs, eG, eGr, tmpg = [ab[:NT, i, :] for i in range(11)]

            def delta_head(h):
                B = HB[h]
                qc, pb = divmod(h, 2)
                pbs = slice(pb * 64, (pb + 1) * 64)
                qT_h = qkn[pbs, qc, :NT]
                kT_h = qkn[pbs, 2 + qc, :NT]
                k_tok_h = kv_tok[:NT, h * 64:(h + 1) * 64]
                v_tok_h = kv_tok[:NT, 256 + h * 64:256 + (h + 1) * 64]
                S_h = Sd[pbs, qc, :]
                N = lambda t: t[:NT, :NT]
                psC = BKV(h, 0, 128)
                kb.mm(psC[:NT, 0:64], kT_h, S_h)
                kb.mm(psC[:NT, 64:128], qT_h, S_h)
                yield
                kb.stt(B["t1"][:NT], psC[:NT, 0:64], eG[:, h:h + 1], v_tok_h, ALU.mult, ALU.subtract)
                kb.act(B["QSs"][:NT], psC[:NT, 64:128], AF.Copy, scale=eG[:, h:h + 1])
                PR(psC)
                kb.ts("dve", B["Kd"][:NT], k_tok_h, eGr[:, h:h + 1], ALU.mult)
                yield
                kb.ts("dve", B["rhsv"][:NT], B["t1"][:NT], nbeta[:, h:h + 1], ALU.mult)
                if NT > 1:
                    kb.ts("dve", N(B["diagG"]), ident[:NT, :NT], Gs[:, h:h + 1], ALU.mult)
                    yield
                    psB = BKV(h, 128, 64)
                    kb.mm(psB[:NT, 0:NT], N(B["diagG"]), ones[:NT, :NT], start=True, stop=False)
                    kb.mm(psB[:NT, 0:NT], C["negones"][:NT, :NT], N(B["diagG"]), start=False, stop=True)
                    yield
                    kb.tt("dve", N(B["dmm"]), psB[:NT, 0:NT], C["negmask"][:NT, :NT], ALU.add)
                    PR(psB)
                    yield
                    kb.act(N(B["dm"]), N(B["dmm"]), AF.Exp)
                    psA = BKV(h, 192, 128)
                    kb.mm(psA[:NT, 0:NT], kT_h, kT_h)
                    kb.mm(psA[:NT, 64:64 + NT], qT_h, kT_h)
                    yield
                    kb.stt(N(B["Araw"]), psA[:NT, 0:NT], beta[:, h:h + 1], N(B["dm"]), ALU.mult, ALU.mult)
                    kb.tt("dve", N(B["Aqk"]), psA[:NT, 64:64 + NT], N(B["dm"]), ALU.mult)
                    PR(psA)
                    yield
                    kb.tt("dve", N(B["Nm0"]), N(B["Araw"]), C["negstrict"][:NT, :NT], ALU.mult)
                    yield
                    psT = BKV(h, 320, 128)
                    kb.tr(psT[:NT, 0:NT], N(B["Nm0"]), ident[:NT, :NT])
                    kb.tr(psT[:NT, 64:64 + NT], N(B["Aqk"]), ident[:NT, :NT])
                    yield
                    kb.copy("act", N(B["Nt0"]), psT[:NT, 0:NT])
                    kb.copy("act", N(B["AqkT"]), psT[:NT, 64:64 + NT])
                    PR(psT)
                    yield
                    kb.tt("dve", N(B["Tt"]), ident[:NT, :NT], N(B["Nt0"]), ALU.add)
                    cur = 0
                    for lev in range(1, 6):
                        nxt = 1 - cur
                        Pc, Ptc, Pn, Ptn = B["Nm%d" % cur], B["Nt%d" % cur], B["Nm%d" % nxt], B["Nt%d" % nxt]
                        psP = BKV(h, 0, 128)
                        kb.mm(psP[:NT, 0:NT], N(Ptc), N(Pc))
                        if lev < 5:
                            kb.mm(psP[:NT, 64:64 + NT], N(Pc), N(Ptc))
                        yield
                        kb.copy("act", N(Pn), psP[:NT, 0:NT])
                        if lev < 5:
                            kb.copy("act", N(Ptn), psP[:NT, 64:64 + NT])
                        PR(psP)
                        yield
                        psU = BKV(h, 128, 64)
                        kb.mm(psU[:NT, 0:NT], N(Pn), N(B["Tt"]))
                        yield
                        kb.tt("dve", N(B["Tt"]), N(B["Tt"]), psU[:NT, 0:NT], ALU.add)
                        PR(psU)
                        yield
                        cur = nxt
                    Tt_use = N(B["Tt"])
                else:
                    psA = BKV(h, 192, 128)
                    kb.mm(psA[:NT, 64:64 + NT], qT_h, kT_h)
                    yield
                    kb.copy("act", N(B["AqkT"]), psA[:NT, 64:64 + NT])
                    PR(psA)
                    Tt_use = ident[:NT, :NT]
                    yield
                psD = BKV(h, 256, 128)
                kb.mm(psD[:NT, 0:64], Tt_use, B["rhsv"][:NT])
                yield
                kb.copy("act", B["vnew"][:NT], psD[:NT, 0:64])
                yield
                kb.mm(psD[:NT, 64:128], N(B["AqkT"]), B["vnew"][:NT])
                psE = BKV(h, 384, 64)
                kb.mm(psE[pbs, 0:64], B["Kd"][:NT], B["vnew"][:NT])
                yield
                kb.tt("dve", o_tok[:NT, h * 64:(h + 1) * 64], B["QSs"][:NT], psD[:NT, 64:128], ALU.add)
                kb.stt(S_h, S_h, dl[pbs, h:h + 1], psE[pbs, 0:64], ALU.mult, ALU.add)
                PR(psD)
                PR(psE)

            def delta_lane():
                for k in range(6):
                    kb.ts("dve", qkvc[:, k, :NT], full_d[:, k, 0:NT], dconv_w[:, k, 0:1], ALU.mult)
                    for j in range(1, 4):
                        kb.stt(qkvc[:, k, :NT], full_d[:, k, j:j + NT], dconv_w[:, k, j:j + 1], qkvc[:, k, :NT],
                               ALU.mult, ALU.add)
                    if k % 2 == 1:
                        yield
                kb.act(qkvs[:, :, :NT], qkvc[:, :, :NT], AF.Silu)
                kb.act(beta, b_t, AF.Sigmoid)
                kb.tt("dve", xsp, a_t, dtb[:NT], ALU.add)
                yield
                kb.tt("dve", sq4[:, :, :NT], qkvs[:, 0:4, :NT], qkvs[:, 0:4, :NT], ALU.mult)
                kb.ts("dve", nbeta, beta, -1.0, ALU.mult)
                kb.ts("dve", axs, xsp, -1.0, ALU.mult)
                kb.tt("dve", axs, axs, xsp, ALU.max)
                yield
                ps = BKV(0, 0, 256)
                for k in range(4):
                    kb.mm(ps[:, k * NT:(k + 1) * NT], C["blockones"], sq4[:, k, :NT])
                kb.act(esp, axs, AF.Exp, scale=-1.0)
                yield
                kb.act(rinv[:, :, :NT], ps[:, 0:4 * NT].r("p (k t) -> p k t", k=4), AF.Sqrt, bias=eps_t)
                PR(ps)
                kb.act(lsp, esp, AF.Ln, bias=1.0)
                kb.ts("dve", xsp, xsp, 0.0, ALU.max)
                yield
                kb.recip(rinv[:, :, :NT], rinv[:, :, :NT])
                kb.tt("dve", lsp, lsp, xsp, ALU.add)
                kb.tt("dve", gg, lsp, negA[:NT], ALU.mult)
                yield
                kb.stt(qkn[:, 0:2, :NT], qkvs[:, 0:2, :NT], 0.125, rinv[:, 0:2, :NT], ALU.mult, ALU.mult)
                kb.tt("dve", qkn[:, 2:4, :NT], qkvs[:, 2:4, :NT], rinv[:, 2:4, :NT], ALU.mult)
                ps2_ = BKV(1, 0, 128)
                kb.mm(ps2_[:NT, 0:4], C["ltriT"][:NT, :NT], gg)
                kb.mm(ps2_[:, 8:12], ones[:NT, :], gg)
                yield
                ps = BKV(2, 0, 512)
                for k in range(2):
                    kb.tr(ps[:NT, k * 128:(k + 1) * 128], qkn[:, 2 + k, :NT], ident)
                    kb.tr(ps[:NT, 256 + k * 128:256 + (k + 1) * 128], qkvs[:, 4 + k, :NT], ident)
                kb.copy("act", Gs, ps2_[:NT, 0:4])
                kb.copy("act", GL, ps2_[:, 8:12])
                PR(ps2_)
                yield
                kb.copy("act", kv_tok[:NT], ps[:NT, 0:512])
                PR(ps)
                kb.act(eG, Gs, AF.Exp)
                kb.act(dl, GL, AF.Exp)
                kb.tt("dve", tmpg, GL[:NT], Gs, ALU.subtract)
                yield
                kb.act(eGr, tmpg, AF.Exp)
                yield
                if DEBUG.get("pairs"):
                    yield from interleave([delta_head(h) for h in (0, 2)])
                    yield from interleave([delta_head(h) for h in (1, 3)])
                else:
                    yield from interleave([delta_head(h) for h in (0, 2, 1, 3)])
                o3 = o_tok[:NT].r("p (h e) -> p h e", h=4)
                q3 = osq[:NT].r("p (h e) -> p h e", h=4)
                kb.tt("dve", osq[:NT], o_tok[:NT], o_tok[:NT], ALU.mult)
                kb.act(ogate[:NT], ztok[:NT, 0:256], AF.Silu)
                yield
                kb.red("dve", ost[:NT, 0:4], q3)
                yield
                kb.act(ost[:NT, 4:8], ost[:NT, 0:4], AF.Sqrt, bias=eps_t[:NT], scale=1.0 / 64)
                yield
                kb.recip(ost[:NT, 4:8], ost[:NT, 4:8])
                kb.tt("dve", q3, o3, ost[:NT, 4:8].un(2).bc([NT, 4, 64]), ALU.mult)
                kb.tt("dve", q3, q3, dnorm[:NT].un(1).bc([NT, 4, 64]), ALU.mult)
                kb.tt("dve", osq[:NT], osq[:NT], ogate[:NT], ALU.mult)
                yield
                ps = BKV(0, 0, 128)
                for k in range(2):
                    kb.tr(ps[:, k * NT:(k + 1) * NT], osq[:NT, k * 128:(k + 1) * 128], ident[:NT, :NT])
                yield
                kb.copy("act", mixT[:, 0:2, :NT], ps[:, 0:2 * NT].r("p (k t) -> p k t", k=2))
                PR(ps)
                if last:
                    hist_out(full_d, 6, 3, NT, D["dc_p"][l] if kind == "p" else D["dc_s"][l, idx], hstage[0])
                    dst = D["d_p"][l] if kind == "p" else D["d_s"][l, idx]
                    for pb_ in range(2):
                        kb.dma("sp", dst.rearrange("(q b) d e -> b d q e", b=2)[pb_], Sd[pb_ * 64:(pb_ + 1) * 64],
                               is_output=True)

            def pool_lane():
                L = 15 + NT
                kb.tt("pool", ps2[:, :, 1:L], full_p[:, :, 1:L], full_p[:, :, 0:L - 1], ALU.add)
                yield
                kb.tt("pool", ps4[:, :, 3:L], ps2[:, :, 3:L], ps2[:, :, 1:L - 2], ALU.add)
                yield
                kb.tt("pool", ps8[:, 7:L], ps4[:, 1, 7:L], ps4[:, 1, 3:L - 4], ALU.add)
                yield
                kb.tt("pool", ps16[64:128, 15:L], ps8[64:128, 15:L], ps8[64:128, 7:L - 8], ALU.add)
                yield
                pinv = C["pool_inv"][:, 0 if first else 1]
                srcs = [(slice(0, 64), 0, ps2[0:64, 0, 15:L]), (slice(64, 128), 0, ps4[64:128, 0, 15:L]),
                        (slice(0, 64), 1, ps8[0:64, 15:L]), (slice(64, 128), 1, ps16[64:128, 15:L])]
                for (psl, k, src) in srcs:
                    kb.tt("pool", pw[psl, k, :NT], src, pinv[psl, k, :NT], ALU.mult)
                yield
                kb.tt("pool", pw[:, :, :NT], pw[:, :, :NT], full_p[:, :, 15:L], ALU.subtract)
                yield
                ps = BKV(6, 256, 128)
                for k in range(2):
                    kb.mm(ps[:, k * NT:(k + 1) * NT], pool_bd[:, k, :], pw[:, k, :NT])
                yield
                for k in range(2):
                    kb.ts("dve", mixT[:, 2 + k, :NT], ps[:, k * NT:(k + 1) * NT], pscale[:, k:k + 1], ALU.mult)
                PR(ps)
                if last:
                    hist_out(full_p, 2, 15, NT, D["pl_p"][l] if kind == "p" else D["pl_s"][l, idx], hstage[1])

            def ret_head(h):
                B = HB[h]
                qc, pb = divmod(h, 2)
                pbs = slice(pb * 64, (pb + 1) * 64)
                v_h = ztok[:NT, 264 + h * 64:264 + (h + 1) * 64]
                S_h = Sr[pbs, qc, :]
                psA = BKV(4 + pb, (h // 2) * 128, 128)
                kb.mm(psA[:NT, 0:NT], rot[pbs, 2 + qc, :NT], rot[pbs, qc, :NT])
                kb.mm(psA[:NT, 64:128], QdT[pbs, qc, :NT], S_h)
                psE = BKV(4, 256 + h * 64, 64)
                kb.mm(psE[pbs, 0:64], Kd_tok[:NT, h * 64:(h + 1) * 64], v_h)
                yield
                kb.tt("dve", B["rAqkT"][:NT, :NT], psA[:NT, 0:NT], C["ret_dmT"][:NT, kk, h, :NT], ALU.mult)
                kb.copy("act", B["rQSs"][:NT], psA[:NT, 64:128])
                kb.stt(S_h, S_h, C["ret_dl"][pbs, kk, h:h + 1], psE[pbs, 0:64], ALU.mult, ALU.add)
                PR(psA)
                PR(psE)
                yield
                psO = BKV(4, 256 + h * 64, 64)
                kb.mm(psO[:NT, 0:64], B["rAqkT"][:NT, :NT], v_h)
                yield
                kb.tt("dve", r_tok[:NT, h * 64:(h + 1) * 64], B["rQSs"][:NT], psO[:NT, 0:64], ALU.add)
                PR(psO)

            def ret_lane():
                ps = BKV(4, 0, 256)
                for k in range(4):
                    kb.mm(ps[:, k * NT:(k + 1) * NT], C["perm"], zT2[:, k, :NT])
                kb.tt("dve", rot1[:, :, :NT], zT2[:, 0:4, :NT], ropeC[:, :NT].un(1).bc([128, 4, NT]), ALU.mult)
                yield
                kb.tt("dve", rot[:, :, :NT], ps[:, 0:4 * NT].r("p (k t) -> p k t", k=4),
                      ropeS[:, :NT].un(1).bc([128, 4, NT]), ALU.mult)
                PR(ps)
                yield
                kb.tt("dve", rot[:, :, :NT], rot[:, :, :NT], rot1[:, :, :NT], ALU.add)
                yield
                kb.ts("dve", rot[:, 2:4, :NT], rot[:, 2:4, :NT], 0.125, ALU.mult)
                kb.tt("dve", QdT[:, :, :NT], rot[:, 0:2, :NT], C["ret_eGrow"][:, kk, :, :NT], ALU.mult)
                yield
                ps = BKV(5, 0, 256)
                for k in range(2):
                    kb.tr(ps[:NT, k * 128:(k + 1) * 128], rot[:, 2 + k, :NT], ident)
                yield
                kb.tt("dve", Kd_tok[:NT].r("p (h e) -> p h e", h=4), ps[:NT, 0:256].r("p (h e) -> p h e", h=4),
                      C["ret_eGr"][:NT, kk, :].un(2).bc([NT, 4, 64]), ALU.mult)
                PR(ps)
                yield
                yield from interleave([ret_head(h) for h in (0, 2, 1, 3)])
                r3 = r_tok[:NT].r("p (h e) -> p h e", h=4)
                c3 = rcen[:NT].r("p (h e) -> p h e", h=4)
                kb.red("dve", rst[:NT, 0:4], r3)
                kb.act(rgate[:NT], ztok[:NT, 520:776], AF.Silu)
                yield
                kb.ts("dve", rst[:NT, 0:4], rst[:NT, 0:4], 1.0 / 64, ALU.mult)
                kb.tt("dve", c3, r3, rst[:NT, 0:4].un(2).bc([NT, 4, 64]), ALU.subtract)
                kb.tt("dve", rsq[:NT], rcen[:NT], rcen[:NT], ALU.mult)
                kb.red("dve", rst[:NT, 0:4], rsq[:NT].r("p (h e) -> p h e", h=4))
                yield
                kb.act(rst[:NT, 4:8], rst[:NT, 0:4], AF.Sqrt, bias=eps_t[:NT], scale=1.0 / 64)
                yield
                kb.recip(rst[:NT, 4:8], rst[:NT, 4:8])
                kb.tt("dve", c3, c3, rst[:NT, 4:8].un(2).bc([NT, 4, 64]), ALU.mult)
                kb.tt("dve", rcen[:NT], rcen[:NT], rgate[:NT], ALU.mult)
                yield
                ps = BKV(4, 0, 128)
                for k in range(2):
                    kb.tr(ps[:, k * NT:(k + 1) * NT], rcen[:NT, k * 128:(k + 1) * 128], ident[:NT, :NT])
                yield
                kb.copy("act", mixT[:, 4:6, :NT], ps[:, 0:2 * NT].r("p (k t) -> p k t", k=2))
                PR(ps)
                if last:
                    dst = D["r_p"][l] if kind == "p" else D["r_s"][l, idx]
                    for pb_ in range(2):
                        kb.dma("sp", dst.rearrange("(q b) d e -> b d q e", b=2)[pb_], Sr[pb_ * 64:(pb_ + 1) * 64],
                               is_output=True)

            def conf_lane():
                kb.act(sg[:, :, :NT], zT2[:, 6:8, :NT], AF.Sigmoid)
                yield
                kb.tt("pool", full_c[:, :, 30:30 + NT], zT2[:, 4:6, :NT], sg[:, :, :NT], ALU.mult)
                yield
                for k in range(2):
                    kb.ts("dve", cacc[:, k, :NT], full_c[:, k, 0:NT], dw_w[:, k, 0:1], ALU.mult, dw_b[:, k:k + 1], ALU.add)
                for j in range(1, 31):
                    for k in range(2):
                        kb.stt(cacc[:, k, :NT], full_c[:, k, j:j + NT], dw_w[:, k, j:j + 1], cacc[:, k, :NT],
                               ALU.mult, ALU.add)
                    if j % 2 == 0:
                        yield
                ps = BKV(6, 0, 64)
                kb.mm(ps[:, 0:NT], ones, cacc[:, 0, :NT], start=True, stop=False)
                kb.mm(ps[:, 0:NT], ones, cacc[:, 1, :NT], start=False, stop=True)
                yield
                for k in range(2):
                    kb.stt(ccen[:, k, :NT], ps[:, 0:NT], -1.0 / 256, cacc[:, k, :NT], ALU.mult, ALU.add)
                PR(ps)
                yield
                kb.tt("pool", csq[:, :, :NT], ccen[:, :, :NT], ccen[:, :, :NT], ALU.mult)
                yield
                ps = BKV(6, 64, 64)
                kb.mm(ps[:, 0:NT], ones, csq[:, 0, :NT], start=True, stop=False)
                kb.mm(ps[:, 0:NT], ones, csq[:, 1, :NT], start=False, stop=True)
                yield
                kb.act(crs[:, :NT], ps[:, 0:NT], AF.Sqrt, bias=eps_t, scale=1.0 / 256)
                PR(ps)
                yield
                kb.recip(crs[:, :NT], crs[:, :NT])
                kb.tt("dve", chn[:, :, :NT], ccen[:, :, :NT], crs[:, :NT].un(1).bc([128, 2, NT]), ALU.mult)
                for k in range(2):
                    kb.ts("dve", chn[:, k, :NT], chn[:, k, :NT], ln_w[:, k:k + 1], ALU.mult, ln_b[:, k:k + 1], ALU.add)
                yield
                kb.act(csq[:, :, :NT], chn[:, :, :NT], AF.Silu)
                yield
                ps = BKV(6, 128, 128)
                for dk in range(2):
                    for ck in range(2):
                        kb.mm(ps[:, dk * NT:(dk + 1) * NT], pw_w[:, ck, dk * 128:(dk + 1) * 128], csq[:, ck, :NT],
                              start=(ck == 0), stop=(ck == 1))
                yield
                kb.copy("act", mixT[:, 6:8, :NT], ps[:, 0:2 * NT].r("p (k t) -> p k t", k=2))
                PR(ps)
                if last:
                    hist_out(full_c, 2, 30, NT, D["c_p"][l] if kind == "p" else D["c_s"][l, idx], hstage[2])

            def ntimes(g, n):
                while True:
                    for _ in range(n):
                        try:
                            next(g)
                        except StopIteration:
                            return
                    yield
            lanes = [ntimes(delta_lane(), DELTA_PRIO), ret_lane(), conf_lane(), pool_lane()]
            if nxt_front is not None:
                def slow(g):
                    for _ in g:
                        yield
                        yield
                lanes.append(slow(nxt_front))
            run_lanes(lanes)

            for nch in range(2):
                ps = BKV(6 + nch, 0, 512)
                for ec in range(8):
                    kb.mm(ps[:NT, 0:512], mixT[:, ec, :NT], W_out[:, ec, nch * 512:(nch + 1) * 512],
                          start=(ec == 0), stop=(ec == 7))
                kb.tt("dve", xo[:NT, nch * 512:(nch + 1) * 512], xA[:NT, nch * 512:(nch + 1) * 512],
                      ps[:NT, 0:512], ALU.add)
                PR(ps)
            kb.dma("sp", xmid[row0:row0 + NT, :], xo[:NT])

        seq_ = [("p", i) for i in range(DEBUG.get("n_ptiles", SEQ // 64))] + \
               [("s", s_) for s_ in range(DEBUG.get("n_stiles", NS))]
        if seq_:
            run_lanes([front_gen(seq_[0][0], seq_[0][1], 0)])
        for t_, (kind_, idx_) in enumerate(seq_):
            nf = front_gen(seq_[t_ + 1][0], seq_[t_ + 1][1], (t_ + 1) % 2) if t_ + 1 < len(seq_) else None
            tile(kind_, idx_, t_ % 2, nf)

    def phaseB(l, xdst, is_last):
        cb = Carver(work, WORKC)
        eidx_all = cb.tile(128, [NTB, 128], I32)
        gate_all = cb.tile(128, [NTB, 128])
        n2bc = cb.tile(128, [1024])
        fbc = cb.tile(128, [1024])
        mark = cb.off
        kb.dma("sp", wq, D["peer_wq"][l].rearrange("(k p) e -> p k e", p=128))
        kb.dma("sp", keysT, D["keysT"][l].rearrange("g c n -> c g n"))
        kb.dma("sp", n2bc, D["norm2"][l:l + 1, :].to_broadcast([128, 1024]))
        if is_last:
            kb.dma("sp", fbc, D["final_norm"][0:1, :].to_broadcast([128, 1024]))
        xB = cb.tile(128, [1024])
        h2 = cb.tile(128, [1024])
        ss = cb.tile(128, [4])
        h2T = cb.tile(128, [8, 128])
        qTg = [cb.tile(128, [128]) for _ in range(2)]
        swork = cb.tile(128, [128])
        vals = cb.tile(128, [16, 16])
        idxu = cb.tile(128, [16, 16], U32)
        idxf = cb.tile(128, [16, 16])
        cwork = cb.tile(128, [256])
        best = cb.tile(128, [8, 16])
        sel = cb.tile(128, [8, 16], U32)
        k1u = cb.tile(128, [128], U32)
        k2u = cb.tile(128, [128], U32)
        k1f = cb.tile(128, [128])
        k2f = cb.tile(128, [128])
        i1 = cb.tile(128, [128])
        i2 = cb.tile(128, [128])
        eidf = cb.tile(128, [128])
        gate = cb.tile(128, [8, 16])
        gsum = cb.tile(128, [8])
        tail = 18432
        s_sbs = [T(wbuf[:, tail:tail + 2048].rearrange("p (a b) -> p a b", a=16)),
                 T(wbuf[:, tail + 6144:tail + 8192].rearrange("p (a b) -> p a b", a=16))]
        cand = T(wbuf[:, tail + 2048:tail + 4096].rearrange("p (a b) -> p a b", a=8))
        oh = T(wbuf[:, tail + 4096:tail + 6144].rearrange("p (a b) -> p a b", a=128))

        def b1_front(ti, row0, NT):
            s_sb = s_sbs[ti % 2]
            kb.dma("sp", xB[:NT], xmid[row0:row0 + NT, :])
            kb.memset("pool", ss[:NT], 0.0)
            kb.act(h2[:NT], xB[:NT], AF.Square, accum=ss[:NT, 0:1])
            kb.act(ss[:NT, 1:2], ss[:NT, 0:1], AF.Sqrt, bias=eps_t[:NT], scale=1.0 / D_MODEL)
            kb.recip(ss[:NT, 2:3], ss[:NT, 1:2])
            kb.stt(h2[:NT], xB[:NT], ss[:NT, 2:3], n2bc[:NT], ALU.mult, ALU.mult)
            for half in range(2):
                ps = BKV(half, 0, 512)
                for kc in range(4):
                    kb.tr(ps[:, kc * NT:(kc + 1) * NT], h2[:NT, (half * 4 + kc) * 128:(half * 4 + kc + 1) * 128],
                          ident[:NT, :NT])
                kb.copy("act", h2T[:, half * 4:half * 4 + 4, :NT], ps[:, 0:4 * NT].r("p (k t) -> p k t", k=4))
                PR(ps)
                yield
            pss = None
            for g in range(16):
                psq = BKV(2 + g % 4, 0, 128)
                for kc in range(8):
                    kb.mm(psq[:, 0:NT], wq[:, kc, g * 128:(g + 1) * 128], h2T[:, kc, :NT],
                          start=(kc == 0), stop=(kc == 7))
                qt = qTg[g % 2]
                kb.copy("act", qt[:, :NT], psq[:, 0:NT])
                PR(psq)
                if g % 4 == 0:
                    pss = BKV(6 + (g // 4) % 2, 0, 512)
                kb.mm(pss[:NT, (g % 4) * 128:(g % 4 + 1) * 128], qt[:, :NT], keysT[:, g, :])
                if g % 4 == 3:
                    kb.copy("act", s_sb[:NT, g - 3:g + 1, :], pss[:NT, 0:512].r("p (g n) -> p g n", g=4))
                    PR(pss)
                yield

        def b1_topk(ti, row0, NT):
            s_sb = s_sbs[ti % 2]
            for g in range(16):
                kb.op("dve", lambda e, g=g: e.max(out=vals.ap[:NT, g, 0:8], in_=s_sb.ap[:NT, g, :]),
                      reads=[s_sb], writes=[vals])
                kb.op("dve", lambda e, g=g: e.max_index(out=idxu.ap[:NT, g, 0:8], in_max=vals.ap[:NT, g, 0:8],
                                                        in_values=s_sb.ap[:NT, g, :]),
                      reads=[s_sb, vals], writes=[idxu])
                kb.op("dve", lambda e, g=g: e.match_replace(out=swork.ap[:NT], in_to_replace=vals.ap[:NT, g, 0:8],
                                                            in_values=s_sb.ap[:NT, g, :], imm_value=NEG),
                      reads=[s_sb, vals], writes=[swork])
                kb.op("dve", lambda e, g=g: e.max(out=vals.ap[:NT, g, 8:16], in_=swork.ap[:NT]),
                      reads=[swork], writes=[vals])
                kb.op("dve", lambda e, g=g: e.max_index(out=idxu.ap[:NT, g, 8:16], in_max=vals.ap[:NT, g, 8:16],
                                                        in_values=swork.ap[:NT]),
                      reads=[swork, vals], writes=[idxu])
                yield
            kb.copy("dve", idxf[:NT], idxu[:NT])
            v4 = vals[:NT].r("p (h two) k -> p h two k", two=2)
            i4 = idxf[:NT].r("p (h two) k -> p h two k", two=2)
            cand4 = cand[:NT].r("p h (a b) -> p h a b", a=16)
            for h in range(8):
                kb.tt("dve", cand4[:, h], v4[:, h, 0, :].un(2).bc([NT, 16, 16]), v4[:, h, 1, :].un(1).bc([NT, 16, 16]),
                      ALU.add)
            for h in range(8):
                kb.op("dve", lambda e, h=h: e.max(out=best.ap[:NT, h, 0:8], in_=cand.ap[:NT, h, :]),
                      reads=[cand], writes=[best])
                kb.op("dve", lambda e, h=h: e.max_index(out=sel.ap[:NT, h, 0:8], in_max=best.ap[:NT, h, 0:8],
                                                        in_values=cand.ap[:NT, h, :]),
                      reads=[cand, best], writes=[sel])
                kb.op("dve", lambda e, h=h: e.match_replace(out=cwork.ap[:NT], in_to_replace=best.ap[:NT, h, 0:8],
                                                            in_values=cand.ap[:NT, h, :], imm_value=NEG),
                      reads=[cand, best], writes=[cwork])
                kb.op("dve", lambda e, h=h: e.max(out=best.ap[:NT, h, 8:16], in_=cwork.ap[:NT]),
                      reads=[cwork], writes=[best])
                kb.op("dve", lambda e, h=h: e.max_index(out=sel.ap[:NT, h, 8:16], in_max=best.ap[:NT, h, 8:16],
                                                        in_values=cwork.ap[:NT]),
                      reads=[cwork, best], writes=[sel])
                yield
            self_flat = sel[:NT].r("p h k -> p (h k)")
            kb.op("dve", lambda e: e.tensor_single_scalar(out=k1u.ap[:NT], in_=self_flat.ap, scalar=4,
                                                          op=ALU.logical_shift_right), reads=[sel], writes=[k1u])
            kb.op("dve", lambda e: e.tensor_single_scalar(out=k2u.ap[:NT], in_=self_flat.ap, scalar=15,
                                                          op=ALU.bitwise_and), reads=[sel], writes=[k2u])
            kb.copy("dve", k1f[:NT], k1u[:NT])
            kb.copy("dve", k2f[:NT], k2u[:NT])
            iota = C["iota16"]
            for (kf, half, dst) in ((k1f, 0, i1), (k2f, 1, i2)):
                kb.tt("dve", oh[:NT], iota[:NT].un(1).bc([NT, 128, 16]), kf[:NT].un(2).bc([NT, 128, 16]), ALU.is_equal)
                for h in range(8):
                    kb.tt("dve", oh[:NT, h * 16:(h + 1) * 16, :], oh[:NT, h * 16:(h + 1) * 16, :],
                          i4[:, h, half, :].un(1).bc([NT, 16, 16]), ALU.mult)
                kb.red("dve", dst[:NT], oh[:NT])
            kb.stt(eidf[:NT], i1[:NT], 128.0, i2[:NT], ALU.mult, ALU.add)
            if l > 0:
                kb.ts("dve", eidf[:NT], eidf[:NT], float(l * 16384), ALU.add)
            kb.ts("dve", eidf[:NT], eidf[:NT], 0.0, ALU.max, float(DEPTH * 16384 - 1), ALU.min)
            kb.copy("dve", eidx_all[:NT, ti, :], eidf[:NT])
            kb.tt("dve", gate[:NT], best[:NT], best[:NT, :, 0:1].bc([NT, 8, 16]), ALU.subtract)
            kb.act(gate[:NT], gate[:NT], AF.Exp)
            kb.red("dve", gsum[:NT], gate[:NT])
            kb.recip(gsum[:NT], gsum[:NT])
            kb.tt("dve", gate_all[:NT, ti, :].r("p (h k) -> p h k", h=8), gate[:NT],
                  gsum[:NT].un(2).bc([NT, 8, 16]), ALU.mult)

        tiles = [(i, i * 128, 128) for i in range(SEQ // 128)] + [(SEQ // 128, SEQ, NS)]
        if "b_tiles" in DEBUG:
            tiles = [t_ for t_ in tiles if t_[0] in DEBUG["b_tiles"]]
        run_lanes([b1_front(*tiles[0])])
        for ix in range(len(tiles)):
            lanes = [b1_topk(*tiles[ix])]
            if ix + 1 < len(tiles):
                lanes.append(b1_front(*tiles[ix + 1]))
            run_lanes(lanes)
        kb.barrier()

        cb.off = mark
        xBs = [cb.tile(128, [1024]) for _ in range(2)]
        h2s = [cb.tile(128, [1024]) for _ in range(2)]
        accs = [cb.tile(128, [1024]) for _ in range(2)]
        junk = cb.tile(128, [1024])
        ss2 = [cb.tile(128, [4]) for _ in range(2)]
        a_ts = [cb.tile(128, [128]) for _ in range(2)]
        g1 = cb.tile(128, [128])
        g2 = cb.tile(128, [128])
        w_ts = [cb.tile(128, [128]) for _ in range(2)]
        NRING = WCOLS // 1024
        ring = [T(wbuf[:, i * 1024:(i + 1) * 1024]) for i in range(NRING)]
        rp = [0]

        def nextbuf():
            b = ring[rp[0] % NRING]
            rp[0] += 1
            return b

        def tile2(ti, row0, NT):
            par = ti % 2
            xB, h2, accV, ss, a_t, w_t = xBs[par], h2s[par], accs[par], ss2[par], a_ts[par], w_ts[par]
            kb.dma("sp", xB[:NT], xmid[row0:row0 + NT, :])
            kb.memset("pool", ss[:NT], 0.0)
            kb.act(h2[:NT], xB[:NT], AF.Square, accum=ss[:NT, 0:1])
            kb.act(ss[:NT, 1:2], ss[:NT, 0:1], AF.Sqrt, bias=eps_t[:NT], scale=1.0 / D_MODEL)
            kb.recip(ss[:NT, 2:3], ss[:NT, 1:2])
            kb.stt(h2[:NT], xB[:NT], ss[:NT, 2:3], n2bc[:NT], ALU.mult, ALU.mult)
            kb.memset("pool", a_t[:NT], 0.0)
            for s in range(128):
                ub = nextbuf()
                kb.dma("pool", ub[:NT], D["peer_u"], indirect=eidx_all[:NT, ti, s:s + 1])
                kb.stt(junk[:NT], ub[:NT], 1.0, h2[:NT], ALU.mult, ALU.mult, accum=a_t[:NT, s:s + 1])
            kb.tt("dve", g1[:NT], a_t[:NT], a_t[:NT], ALU.mult)
            kb.ts("dve", g1[:NT], g1[:NT], 0.044715, ALU.mult, 1.0, ALU.add)
            kb.tt("dve", g1[:NT], g1[:NT], a_t[:NT], ALU.mult)
            kb.act(g2[:NT], g1[:NT], AF.Tanh, scale=0.7978845608028654)
            kb.ts("dve", g2[:NT], g2[:NT], 1.0, ALU.add, 0.5, ALU.mult)
            kb.tt("dve", g2[:NT], g2[:NT], a_t[:NT], ALU.mult)
            kb.tt("dve", w_t[:NT], g2[:NT], gate_all[:NT, ti, :], ALU.mult)
            for s in range(128):
                vb = nextbuf()
                kb.dma("pool", vb[:NT], D["peer_v"], indirect=eidx_all[:NT, ti, s:s + 1])
                if s == 0:
                    kb.ts("dve", accV[:NT], vb[:NT], w_t[:NT, 0:1], ALU.mult)
                else:
                    kb.stt(accV[:NT], vb[:NT], w_t[:NT, s:s + 1], accV[:NT], ALU.mult, ALU.add)
            kb.tt("dve", accV[:NT], xB[:NT], accV[:NT], ALU.add)
            if not is_last:
                kb.dma("sp", xdst[row0:row0 + NT, :], accV[:NT])
            else:
                kb.memset("pool", ss[:NT], 0.0)
                kb.act(junk[:NT], accV[:NT], AF.Square, accum=ss[:NT, 0:1])
                kb.act(ss[:NT, 1:2], ss[:NT, 0:1], AF.Sqrt, bias=eps_t[:NT], scale=1.0 / D_MODEL)
                kb.recip(ss[:NT, 2:3], ss[:NT, 1:2])
                kb.stt(accV[:NT], accV[:NT], ss[:NT, 2:3], fbc[:NT], ALU.mult, ALU.mult)
                if row0 < SEQ:
                    kb.dma("sp", D["y_p"][row0:row0 + NT, :], accV[:NT], is_output=True)
                else:
                    kb.dma("sp", D["y_s"][:, :], accV[:NT], is_output=True)

        for (ti, row0, NT) in tiles:
            tile2(ti, row0, NT)

    for l in range(DEPTH):
        if DEBUG.get("only_layer0") and l > 0:
            break
        if l == 0:
            phaseA(l, D["xp"], D["xs"])
        else:
            phaseA(l, xnext[0:SEQ, :], xnext[SEQ:NTOK, :])
        kb.barrier()
        if DEBUG.get("skip_b"):
            continue
        phaseB(l, xnext, l == DEPTH - 1)
        kb.barrier()
    kb.finish()
    return nc, kb


_CACHE = {}


def kernel(x_prompt, x_sample, state_delta, state_delta_conv, state_pool, state_ret, state_conv,
           norm1, w_in, delta_conv_w, delta_a_log, delta_dt_bias, delta_norm_w, pool_w, pool_scale,
           conv_dw_w, conv_dw_b, conv_ln_w, conv_ln_b, conv_pw_w, w_out, norm2,
           peer_wq, peer_keys, peer_u, peer_v, final_norm):
    f = lambda a: np.ascontiguousarray(np.asarray(a, dtype=np.float32))
    if "nc" not in _CACHE:
        _CACHE["nc"] = build_program()
    nc, kb = _CACHE["nc"]
    consts = host_constants()
    keysT = f(np.transpose(np.asarray(peer_keys), (0, 1, 2, 4, 3)).reshape(DEPTH, 16, 128, 128))
    dconv_wT = f(np.transpose(np.asarray(delta_conv_w), (0, 2, 1)))
    dw_wT = f(np.transpose(np.asarray(conv_dw_w), (0, 2, 1)))
    cv = np.stack([np.asarray(conv_dw_b), np.asarray(conv_ln_w), np.asarray(conv_ln_b), np.asarray(pool_scale)], 1)
    colvecs = f(np.transpose(cv.reshape(DEPTH, 4, 2, 128), (0, 3, 1, 2)).reshape(DEPTH, 128, 8))
    shared = dict(
        norm1=f(norm1), norm2=f(norm2), final_norm=f(np.asarray(final_norm).reshape(1, D_MODEL)),
        w_in=f(w_in), w_out=f(w_out), peer_wq=f(peer_wq), keysT=keysT, peer_u=f(peer_u).reshape(DEPTH * 16384, D_MODEL), peer_v=f(peer_v).reshape(DEPTH * 16384, D_MODEL),
        dconv_wT=dconv_wT, dw_wT=dw_wT, colvecs=colvecs, pw_w=f(conv_pw_w), pool_w=f(pool_w),
        a_log=f(delta_a_log), dt_bias=f(delta_dt_bias), dnorm_w=f(delta_norm_w),
    )
    for k, v in consts.items():
        shared["c_" + k] = f(v)
    xp = np.asarray(x_prompt)
    xs = np.asarray(x_sample)
    in_maps = []
    for c in range(NCORES):
        sl = slice(c * NS, (c + 1) * NS)
        m = dict(shared)
        m["xp"] = f(xp[c])
        m["xs"] = f(xs[sl, 0, :])
        m["st_delta"] = f(np.asarray(state_delta)[:, sl])
        m["st_dconv"] = f(np.asarray(state_delta_conv)[:, sl])
        m["st_pool"] = f(np.asarray(state_pool)[:, sl])
        m["st_ret"] = f(np.asarray(state_ret)[:, sl])
        m["st_conv"] = f(np.asarray(state_conv)[:, sl])
        in_maps.append(m)
    res = run_bass_kernel_spmd(nc, in_maps, core_ids=list(range(NCORES)))
    R = res.results
    y_p = np.stack([R[c]["y_p"] for c in range(NCORES)], 0)
    y_s = np.concatenate([R[c]["y_s"] for c in range(NCORES)], 0)[:, None, :]
    outs = [y_p, y_s]
    for nm in ("d_p", "dc_p", "pl_p", "r_p", "c_p"):
        outs.append(np.stack([R[c][nm] for c in range(NCORES)], 1))
    for nm in ("d_s", "dc_s", "pl_s", "r_s", "c_s"):
        outs.append(np.concatenate([R[c][nm] for c in range(NCORES)], 1))
    return tuple(np.ascontiguousarray(o.astype(np.float32)) for o in outs)
```
